# Optimizing a Trainium2 kernel written in Bass

```python
import math
import jax, jax.numpy as jnp
from jax import lax
import numpy as np

D_MODEL = 2048
BATCH = 8
SEQ = 4096
DEPTH = 2

N_A = (DEPTH + 1) // 2
N_B = DEPTH - N_A
HEAD_DIM = 128
N_HEADS = D_MODEL // HEAD_DIM
NSA_KV_GROUPS = 4
B_KV_HEADS = 4
CMP_BLOCK = 32
CMP_STRIDE = 16
CMP_HIDDEN = HEAD_DIM
SEL_BLOCK = 64
SEL_COUNT = 16
SEL_FORCE = 1e4
NSA_WINDOW = 512
NSA_Q_CHUNK = 32
DIL_PAIRS = ((128, 1), (512, 4), (2048, 16))
D_FF = 5632
CONV_WIDTH = 3
REL_BUCKETS = 32
REL_MAX_DIST = 4096
QUERY_BLOCK = 128
EPS = 1e-6
SCALE = HEAD_DIM ** -0.5
A_IN_COLS = N_HEADS * HEAD_DIM + 6 * NSA_KV_GROUPS * HEAD_DIM + 3 * N_HEADS

kernel_name = "yoco_nsa_dilated_convffn_trunk"


def rmsnorm(x, g):
    xf = x.astype(jnp.float32)
    y = xf * lax.rsqrt(jnp.mean(xf * xf, axis=-1, keepdims=True) + EPS) * g.astype(jnp.float32)
    return y.astype(x.dtype)


def head_rms(x, g):
    return rmsnorm(x, g)


def t5_bucket(dist):
    n = jnp.maximum(jnp.asarray(dist, jnp.int32), 0)
    exact = REL_BUCKETS // 2
    log_ratio = jnp.log(jnp.maximum(n, 1).astype(jnp.float32) / exact) / math.log(REL_MAX_DIST / exact)
    large = exact + (log_ratio * (REL_BUCKETS - exact)).astype(jnp.int32)
    return jnp.where(n < exact, n, jnp.minimum(large, REL_BUCKETS - 1))


def masked_softmax(logits, mask):
    s = jnp.where(mask, logits.astype(jnp.float32), -jnp.inf)
    m = jnp.max(s, axis=-1, keepdims=True)
    m = jnp.where(jnp.isfinite(m), m, 0.0)
    e = jnp.exp(s - m)
    den = jnp.sum(e, axis=-1, keepdims=True)
    p = e / jnp.maximum(den, 1e-30)
    lse = (m + jnp.log(den))[..., 0]
    return p, lse


def banded_attention(q, k, v, rel_table, n_back, dilation):
    n, length, h, dh = q.shape
    g = k.shape[2]
    r = h // g
    qb = math.gcd(length, QUERY_BLOCK)
    nb = length // qb
    span = qb + n_back
    pad = ((0, 0), (n_back, 0), (0, 0), (0, 0))
    k_pad, v_pad = jnp.pad(k, pad), jnp.pad(v, pad)
    idx = np.arange(nb)[:, None] * qb + np.arange(span)[None, :]
    k_blk, v_blk = k_pad[:, idx], v_pad[:, idx]
    q_blk = q.reshape(n, nb, qb, g, r, dh)
    logits = jnp.einsum('nbqgrd,nbsgd->nbgrqs', q_blk, k_blk).astype(jnp.float32) * SCALE
    steps = np.arange(qb)[:, None] + n_back - np.arange(span)[None, :]
    key_pos = idx - n_back
    mask = (steps >= 0) & (steps <= n_back) & (key_pos[:, None, :] >= 0)
    bias = rel_table[t5_bucket(steps * dilation)]
    bias = bias.reshape(qb, span, g, r).transpose(2, 3, 0, 1)
    p, lse = masked_softmax(logits + bias, mask[None, :, None, None])
    out = jnp.einsum('nbgrqs,nbsgd->nbqgrd', p, v_blk)
    return out.reshape(n, length, h, dh), lse.transpose(0, 1, 4, 2, 3).reshape(n, length, h)


def compress(kv, pos_emb, w1, w2):
    b, s, g, dh = kv.shape
    n_cmp = (s - CMP_BLOCK) // CMP_STRIDE + 1
    idx = np.arange(n_cmp)[:, None] * CMP_STRIDE + np.arange(CMP_BLOCK)[None, :]
    blocks = kv[:, idx] + pos_emb[:, None, :]
    flat = blocks.transpose(0, 1, 3, 2, 4).reshape(b, n_cmp, g, CMP_BLOCK * dh)
    return jax.nn.gelu(flat @ w1) @ w2


def block_overlap(n_cmp, n_blk):
    start = np.arange(n_cmp)[:, None] * CMP_STRIDE
    blk_start = np.arange(n_blk)[None, :] * SEL_BLOCK
    ov = np.minimum(start + CMP_BLOCK, blk_start + SEL_BLOCK) - np.maximum(start, blk_start)
    return np.maximum(ov, 0).astype(np.float32) / CMP_BLOCK


def nsa_compressed_selected(q, k_cmp, v_cmp, k_slc, v_slc, rel_table):
    b, s, h, dh = q.shape
    g = k_cmp.shape[2]
    r = h // g
    n_cmp = k_cmp.shape[1]
    n_blk = s // SEL_BLOCK
    n_sel = min(SEL_COUNT, n_blk)
    qc = NSA_Q_CHUNK
    n_chunk = s // qc
    comp_end = (np.arange(n_cmp) * CMP_STRIDE + CMP_BLOCK - 1).astype(np.int32)
    overlap = jnp.asarray(block_overlap(n_cmp, n_blk))
    k_blocks = k_slc.reshape(b, n_blk, SEL_BLOCK, g, dh).transpose(0, 3, 1, 2, 4)
    v_blocks = v_slc.reshape(b, n_blk, SEL_BLOCK, g, dh).transpose(0, 3, 1, 2, 4)
    table_g = rel_table.reshape(REL_BUCKETS, g, r).transpose(1, 0, 2)
    b_idx = jnp.arange(b)[:, None, None, None]
    g_idx = jnp.arange(g)[None, :, None, None]
    blk = jnp.arange(n_blk, dtype=jnp.int32)

    def chunk(args):
        q_c, t0 = args
        t = t0 + jnp.arange(qc, dtype=jnp.int32)
        logit_c = jnp.einsum('bqgrd,bngd->bgrqn', q_c, k_cmp).astype(jnp.float32) * SCALE
        dist_c = t[:, None] - comp_end[None, :]
        bias_c = rel_table[t5_bucket(dist_c)].reshape(qc, n_cmp, g, r).transpose(2, 3, 0, 1)
        p_c, _ = masked_softmax(logit_c + bias_c, dist_c >= 0)
        o_c = jnp.einsum('bgrqn,bngd->bqgrd', p_c, v_cmp)
        imp = jnp.einsum('bgrqn,nj->bgqj', p_c, overlap)
        cur = (t // SEL_BLOCK)[:, None]
        forced = (blk == 0) | (blk == cur) | (blk == cur - 1)
        score = jnp.where(blk * SEL_BLOCK <= t[:, None], imp + jnp.where(forced, SEL_FORCE, 0.0), -jnp.inf)
        _, sel = lax.top_k(score, n_sel)
        k_s = k_blocks[b_idx, g_idx, sel].reshape(b, g, qc, n_sel * SEL_BLOCK, dh)
        v_s = v_blocks[b_idx, g_idx, sel].reshape(b, g, qc, n_sel * SEL_BLOCK, dh)
        pos = (sel[..., None] * SEL_BLOCK + jnp.arange(SEL_BLOCK, dtype=jnp.int32)).reshape(b, g, qc, -1)
        dist_s = t[:, None] - pos
        logit_s = jnp.einsum('bqgrd,bgqsd->bgrqs', q_c, k_s).astype(jnp.float32) * SCALE
        bias_s = jnp.moveaxis(table_g[g_idx, t5_bucket(dist_s)], -1, 2)
        p_s, _ = masked_softmax(logit_s + bias_s, (dist_s >= 0)[:, :, None])
        o_s = jnp.einsum('bgrqs,bgqsd->bqgrd', p_s, v_s)
        return o_c, o_s

    q_chunks = q.reshape(b, n_chunk, qc, g, r, dh).swapaxes(0, 1)
    starts = jnp.arange(n_chunk, dtype=jnp.int32) * qc
    o_c, o_s = lax.map(chunk, (q_chunks, starts))
    return (o_c.swapaxes(0, 1).reshape(b, s, h, dh), o_s.swapaxes(0, 1).reshape(b, s, h, dh))


def nsa_mixer(h, rel_table, w_in, q_gain, k_gains, cmp_pos, cmp_w1, cmp_w2, w_out):
    b, s, _ = h.shape
    g, dh = NSA_KV_GROUPS, HEAD_DIM
    q_dim, kv_dim = N_HEADS * dh, g * dh
    proj = h @ w_in
    q = head_rms(proj[..., :q_dim].reshape(b, s, N_HEADS, dh), q_gain)
    kvs = proj[..., q_dim:q_dim + 6 * kv_dim].reshape(b, s, 6, g, dh)
    gates = jax.nn.sigmoid(proj[..., q_dim + 6 * kv_dim:].astype(jnp.float32)).reshape(b, s, N_HEADS, 3)
    k_cmp = head_rms(compress(kvs[:, :, 0], cmp_pos[0], cmp_w1[0], cmp_w2[0]), k_gains[0])
    v_cmp = compress(kvs[:, :, 1], cmp_pos[1], cmp_w1[1], cmp_w2[1])
    k_slc, v_slc = head_rms(kvs[:, :, 2], k_gains[1]), kvs[:, :, 3]
    k_win, v_win = head_rms(kvs[:, :, 4], k_gains[2]), kvs[:, :, 5]
    o_cmp, o_slc = nsa_compressed_selected(q, k_cmp, v_cmp, k_slc, v_slc, rel_table)
    o_win, _ = banded_attention(q, k_win, v_win, rel_table, NSA_WINDOW - 1, 1)
    o = gates[..., 0:1] * o_cmp + gates[..., 1:2] * o_slc + gates[..., 2:3] * o_win
    return o.reshape(b, s, q_dim).astype(h.dtype) @ w_out


def shared_kv(x, g_norm, w_kv, k_gain):
    b, s, _ = x.shape
    kv = (rmsnorm(x, g_norm) @ w_kv).reshape(b, s, 2, B_KV_HEADS, HEAD_DIM)
    return head_rms(kv[:, :, 0], k_gain), kv[:, :, 1]


def to_residues(x, d):
    b, s = x.shape[:2]
    y = jnp.moveaxis(x.reshape(b, s // d, d, *x.shape[2:]), 2, 1)
    return y.reshape(b * d, s // d, *x.shape[2:])


def from_residues(x, b, d):
    n, length = x.shape[:2]
    y = jnp.moveaxis(x.reshape(b, d, length, *x.shape[2:]), 1, 2)
    return y.reshape(b, length * d, *x.shape[2:])


def dilated_mixer(h, k_sh, v_sh, rel_table, w_q, q_gains, w_out):
    b, s, _ = h.shape
    q = (h @ w_q).reshape(b, s, len(DIL_PAIRS), N_HEADS, HEAD_DIM)
    outs, lses = [], []
    for gi, (window, dil) in enumerate(DIL_PAIRS):
        qg = to_residues(head_rms(q[:, :, gi], q_gains[gi]), dil)
        o, lse = banded_attention(qg, to_residues(k_sh, dil), to_residues(v_sh, dil), rel_table, window // dil, dil)
        outs.append(from_residues(o, b, dil))
        lses.append(from_residues(lse, b, dil))
    alpha = jax.nn.softmax(jnp.stack(lses), axis=0)
    o = jnp.einsum('gbsh,gbshd->bshd', alpha, jnp.stack(outs))
    return o.reshape(b, s, N_HEADS * HEAD_DIM).astype(h.dtype) @ w_out


def conv_ffn(x, g, w_up, conv_w, w_down):
    u = rmsnorm(x, g) @ w_up
    u = lax.conv_general_dilated(u, conv_w[:, None, :], window_strides=(1,),
                                 padding=[(CONV_WIDTH - 1, 0)],
                                 dimension_numbers=('NWC', 'WIO', 'NWC'),
                                 feature_group_count=u.shape[-1])
    gate, val = jnp.split(u, 2, axis=-1)
    return (jax.nn.silu(gate) * val) @ w_down


def setup_inputs(seed: int = 0) -> dict:
    key = jax.random.key(seed)
    ks = jax.random.split(key, 20)
    f32 = jnp.float32

    def nrm(k, shape, fan_in):
        return jax.random.normal(k, shape, f32) * (fan_in ** -0.5)

    def gain(k, shape):
        return 1.0 + 0.05 * jax.random.normal(k, shape, f32)

    dh = HEAD_DIM
    return {
        'x': jax.random.normal(ks[0], (BATCH, SEQ, D_MODEL), f32),
        'rel_bias': 0.3 * jax.random.normal(ks[1], (REL_BUCKETS, N_HEADS), f32),
        'attn_norm': gain(ks[2], (DEPTH, D_MODEL)),
        'ffn_norm': gain(ks[3], (DEPTH, D_MODEL)),
        'a_w_in': nrm(ks[4], (N_A, D_MODEL, A_IN_COLS), D_MODEL),
        'a_q_norm': gain(ks[5], (N_A, dh)),
        'a_k_norm': gain(ks[6], (N_A, 3, dh)),
        'a_cmp_pos': 0.1 * jax.random.normal(ks[7], (N_A, 2, CMP_BLOCK, dh), f32),
        'a_cmp_w1': nrm(ks[8], (N_A, 2, CMP_BLOCK * dh, CMP_HIDDEN), CMP_BLOCK * dh),
        'a_cmp_w2': nrm(ks[9], (N_A, 2, CMP_HIDDEN, dh), CMP_HIDDEN),
        'a_w_out': nrm(ks[10], (N_A, N_HEADS * dh, D_MODEL), N_HEADS * dh),
        'kv_norm': gain(ks[11], (D_MODEL,)),
        'kv_w': nrm(ks[12], (D_MODEL, 2 * B_KV_HEADS * dh), D_MODEL),
        'kv_k_norm': gain(ks[13], (dh,)),
        'b_w_q': nrm(ks[14], (N_B, D_MODEL, len(DIL_PAIRS) * N_HEADS * dh), D_MODEL),
        'b_q_norm': gain(ks[15], (N_B, len(DIL_PAIRS), dh)),
        'b_w_out': nrm(ks[16], (N_B, N_HEADS * dh, D_MODEL), N_HEADS * dh),
        'ffn_w_up': nrm(ks[17], (DEPTH, D_MODEL, 2 * D_FF), D_MODEL),
        'ffn_conv': nrm(ks[18], (DEPTH, CONV_WIDTH, 2 * D_FF), CONV_WIDTH),
        'ffn_w_down': nrm(ks[19], (DEPTH, D_FF, D_MODEL), D_FF),
    }


def reference(x, rel_bias, attn_norm, ffn_norm, a_w_in, a_q_norm, a_k_norm, a_cmp_pos, a_cmp_w1,
              a_cmp_w2, a_w_out, kv_norm, kv_w, kv_k_norm, b_w_q, b_q_norm, b_w_out,
              ffn_w_up, ffn_conv, ffn_w_down):
    k_sh, v_sh = None, None
    for layer in range(DEPTH):
        h = rmsnorm(x, attn_norm[layer])
        if layer < N_A:
            mix = nsa_mixer(h, rel_bias, a_w_in[layer], a_q_norm[layer], a_k_norm[layer],
                            a_cmp_pos[layer], a_cmp_w1[layer], a_cmp_w2[layer], a_w_out[layer])
        else:
            if layer == N_A:
                k_sh, v_sh = shared_kv(x, kv_norm, kv_w, kv_k_norm)
            j = layer - N_A
            mix = dilated_mixer(h, k_sh, v_sh, rel_bias, b_w_q[j], b_q_norm[j], b_w_out[j])
        x = x + mix.astype(x.dtype)
        x = x + conv_ffn(x, ffn_norm[layer], ffn_w_up[layer], ffn_conv[layer], ffn_w_down[layer]).astype(x.dtype)
    return x
```

```python
import contextlib
import math
import numpy as np
import concourse.bass as bass
import concourse.mybir as mybir
from concourse.bass_utils import run_bass_kernel_spmd

F32 = mybir.dt.float32
BF16 = mybir.dt.bfloat16
AF = mybir.ActivationFunctionType
ALU = mybir.AluOpType

S = 4096
D = 2048
H = 16
DH = 128
G = 4
DFF = 5632
NCMP = 255
EPS = 1e-6
SCALE = DH ** -0.5
A_IN = 5168
GLEN = 6784
GOFF = 2560
NEG = 240.0


class Buf:
    def __init__(self, ap):
        self.ap = ap
        self.w = {}
        self.r = {}
        self.ds = None
        self.bg = False


class Trk:
    def __init__(self, nc, es):
        self.nc = nc
        self.es = es
        self.engs = {}
        for n in ("tensor", "vector", "scalar", "gpsimd", "sync"):
            sem = es.enter_context(nc.semaphore("e_" + n))
            self.engs[n] = dict(h=getattr(nc, n), sem=sem, cnt=0, waited={}, pend=False)
        self.dpool = []
        self.dlive = []
        self.nd = 0

    def _wait(self, eng, deps):
        e = self.engs[eng]
        for k, (sem, val) in deps.items():
            if eng == "tensor" and k == "tensor":
                continue
            if e["waited"].get(k, 0) < val:
                e["h"].wait_ge(sem, val)
                e["waited"][k] = val

    @staticmethod
    def _deps(reads, writes):
        d = {}
        for b in reads:
            for k, sv in b.w.items():
                if k not in d or d[k][1] < sv[1]:
                    d[k] = sv
        for b in writes:
            for src in (b.w, b.r):
                for k, sv in src.items():
                    if k not in d or d[k][1] < sv[1]:
                        d[k] = sv
        return d

    def op(self, eng, fn, reads=(), writes=(), inc=True, skip_self=False):
        e = self.engs[eng]
        d = self._deps(reads, writes)
        if skip_self:
            d.pop(eng, None)
        self._wait(eng, d)
        ins = fn(e["h"])
        if inc:
            e["cnt"] += 1
            ins.then_inc(e["sem"], 1)
            val = e["cnt"]
        else:
            val = e["cnt"] + 1
        for b in writes:
            b.w = {eng: (e["sem"], val)}
            b.r = {}
        for b in reads:
            b.r[eng] = (e["sem"], val)
        return ins

    def _dsem(self, b):
        if b.ds is None:
            if self.dpool:
                b.ds = self.dpool.pop()
            else:
                sem = self.es.enter_context(self.nc.semaphore("d%d" % self.nd))
                b.ds = [sem, 0, "d%d" % self.nd]
                self.nd += 1
            self.dlive.append(b)
        return b.ds

    def dma(self, eng, out, in_, reads=(), writes=()):
        e = self.engs[eng]
        self._wait(eng, self._deps(reads, writes))
        ins = e["h"].dma_start(out=out, in_=in_)
        bs = list(reads) + list(writes)
        assert len(bs) == 1
        b = bs[0]
        ds = self._dsem(b)
        ds[1] += 16
        ins.then_inc(ds[0], 16)
        if writes:
            b.w = {ds[2]: (ds[0], ds[1])}
            b.r = {}
        else:
            b.r[ds[2]] = (ds[0], ds[1])
        return ins

    def barrier(self, allbg=False):
        sp = self.engs["sync"]
        keep = []
        for b in self.dlive:
            if b.bg and not allbg:
                keep.append(b)
                continue
            ds = b.ds
            if sp["waited"].get(ds[2], 0) < ds[1]:
                sp["h"].wait_ge(ds[0], ds[1])
                sp["waited"][ds[2]] = ds[1]
            self.dpool.append(ds)
            b.ds = None
        self.dlive = keep
        names = list(self.engs)
        for n in names:
            e = self.engs[n]
            for m in names:
                if m == n or m == "sync":
                    continue
                o = self.engs[m]
                if e["waited"].get(m, 0) < o["cnt"]:
                    e["h"].wait_ge(o["sem"], o["cnt"])
                    e["waited"][m] = o["cnt"]
        sp["cnt"] += 1
        sp["h"].sem_inc(sp["sem"], 1)
        for n in names:
            if n == "sync":
                continue
            e = self.engs[n]
            e["h"].wait_ge(sp["sem"], sp["cnt"])
            e["waited"]["sync"] = sp["cnt"]


def ap3(ap, dims):
    return bass.AP(ap.tensor, ap.offset, [list(ap.ap[0])] + [list(d) for d in dims])


class K:
    def __init__(self, dbg=None):
        self.dbg = dbg or {}
        self.nc = bass.Bass("TRN2", target_bir_lowering=False)
        self.dr = {}

    def din(self, name, shape, dt=F32):
        t = self.nc.dram_tensor(name, list(shape), dt, kind="ExternalInput").ap()
        self.dr[name] = t
        return t

    def dscr(self, name, shape, dt):
        kind = "ExternalOutput" if name in self.dbg else "Internal"
        t = self.nc.dram_tensor(name, list(shape), dt, kind=kind).ap()
        self.dr[name] = t
        return t

    def sb(self, es, name, shape, dt):
        self.uid = getattr(self, "uid", 0) + 1
        return es.enter_context(self.nc.sbuf_tensor("%s_u%d" % (name, self.uid), list(shape), dt))

    def phase_norm(self, xT, gname, hT, hparts, pad, psb):
        nc, trk = self.nc, self.trk
        with contextlib.ExitStack() as es:
            xt = [Buf(self.sb(es, "nxt%d" % i, [128, 16, 256], F32)) for i in range(2)]
            sq = [Buf(self.sb(es, "nsq%d" % i, [128, 16, 256], BF16)) for i in range(2)]
            rs = [Buf(self.sb(es, "nrs%d" % i, [128, 256], F32)) for i in range(2)]
            g = Buf(self.sb(es, "ng", [128, 16], F32))
            trk.dma("sync", out=g.ap[:, :], in_=self.dr[gname], writes=[g])
            xv = xT.rearrange("(kc p) t -> p kc t", p=128)
            nt = S // 256
            trk.dma("sync", out=xt[0].ap[:, :, :], in_=xv[:, :, 0:256], writes=[xt[0]])
            for i in range(nt):
                x_, s_, r_ = xt[i % 2], sq[i % 2], rs[i % 2]
                if i + 1 < nt:
                    trk.dma("sync" if i % 2 == 1 else "gpsimd", out=xt[(i + 1) % 2].ap[:, :, :],
                            in_=xv[:, :, (i + 1) * 256:(i + 2) * 256], writes=[xt[(i + 1) % 2]])
                trk.op("scalar", lambda h: h.activation(out=s_.ap[:, :, :], in_=x_.ap[:, :, :], func=AF.Square),
                       reads=[x_], writes=[s_])
                ps = psb[i % 2]
                for kc in range(16):
                    trk.op("tensor", lambda h: h.matmul(ps.ap[:, 0:256], lhsT=self.cD.ap[:, :], rhs=s_.ap[:, kc, :],
                                                        start=(kc == 0), stop=(kc == 15)),
                           reads=[s_, self.cD], writes=[ps], inc=(kc == 15))
                trk.op("scalar", lambda h: h.activation(out=r_.ap[:, :], in_=ps.ap[:, 0:256], func=AF.Ln,
                                                        bias=self.epsc.ap[:, 0:1], scale=1.0),
                       reads=[ps, self.epsc], writes=[r_])
                trk.op("scalar", lambda h: h.activation(out=r_.ap[:, :], in_=r_.ap[:, :], func=AF.Exp, scale=-0.5),
                       reads=[r_], writes=[r_])
                for kc in range(16):
                    trk.op("vector", lambda h: h.scalar_tensor_tensor(out=hT[:, kc, pad + i * 256: pad + (i + 1) * 256],
                                                                      in0=x_.ap[:, kc, :], scalar=g.ap[:, kc:kc + 1],
                                                                      in1=r_.ap[:, :], op0=ALU.mult, op1=ALU.mult),
                           reads=[x_, r_, g], writes=[hparts[i]], skip_self=(kc > 0))
            trk.barrier()

    def hp(self, hparts, pad, t0, n):
        a = max(t0 - pad, 0) // 256
        b = (t0 + n - 1 - pad) // 256
        return [hparts[i] for i in range(a, b + 1)]

    def gemm_fm(self, hT, hparts, pad, KC, W, col_tiles, tok_tiles, epilogue, psA, wt_cols, wname, after=None, barrier=True):
        nc, trk = self.nc, self.trk
        with contextlib.ExitStack() as es:
            wb = [Buf(self.sb(es, "%s_w%d" % (wname, i), [128, KC, wt_cols], BF16)) for i in range(2)]
            Wv = W.rearrange("(kc p) n -> p kc n", p=128)

            def load(ti):
                off = 0
                kstep = 11 if KC == 44 else 16
                for (c0, wd) in col_tiles[ti]:
                    for k0 in range(0, KC, kstep):
                        trk.dma("gpsimd", out=wb[ti % 2].ap[:, k0:k0 + kstep, off:off + wd],
                                in_=Wv[:, k0:k0 + kstep, c0:c0 + wd], writes=[wb[ti % 2]])
                    off += wd

            load(0)
            k = 0
            pend = []
            for ti, pieces in enumerate(col_tiles):
                if ti + 1 < len(col_tiles):
                    load(ti + 1)
                wbuf = wb[ti % 2]
                off = 0
                for (c0, wd) in pieces:
                    for nb in range(0, wd, 128):
                        m = min(128, wd - nb)
                        for tk, (t0, n) in enumerate(tok_tiles):
                            ps = psA[k % len(psA)]
                            k += 1
                            rd = [wbuf] + self.hp(hparts, pad, t0, n)
                            for kc in range(KC):
                                trk.op("tensor", lambda h: h.matmul(ps.ap[0:m, 0:n],
                                                                    lhsT=wbuf.ap[:, kc, off + nb:off + nb + m],
                                                                    rhs=hT[:, kc, t0:t0 + n], start=(kc == 0),
                                                                    stop=(kc == KC - 1)),
                                       reads=rd, writes=[ps], inc=(kc == KC - 1))
                            if pend:
                                pend.pop()()
                            pend.append(lambda a=(es, c0 + nb, m, tk, t0, n, ps): epilogue(*a))
                            if getattr(self, "bgwork", None) and k % 3 == 0:
                                self.bgwork.pop(0)()
                    off += wd
            if pend:
                pend.pop()()
            if after:
                after(es)
            if barrier:
                trk.barrier()

    def gemm_tm(self, hT, hparts, pad, W, c0, out_dram, psA, name):
        nc, trk = self.nc, self.trk
        with contextlib.ExitStack() as es:
            wb = Buf(self.sb(es, name + "_w", [128, 16, 512], BF16))
            Wv = W.rearrange("(kc p) n -> p kc n", p=128)
            trk.dma("gpsimd", out=wb.ap[:, :, :], in_=Wv[:, :, c0:c0 + 512], writes=[wb])
            ob = [Buf(self.sb(es, name + "_o%d" % i, [128, 512], BF16)) for i in range(3)]
            for tt in range(S // 128):
                ps = psA[tt % len(psA)]
                t0 = pad + tt * 128
                rd = [wb] + self.hp(hparts, pad, t0, 128)
                for kc in range(16):
                    trk.op("tensor", lambda h: h.matmul(ps.ap[:, :], lhsT=hT[:, kc, t0:t0 + 128], rhs=wb.ap[:, kc, :],
                                                        start=(kc == 0), stop=(kc == 15)),
                           reads=rd, writes=[ps], inc=(kc == 15))
                o = ob[tt % 3]
                trk.op("scalar", lambda h: h.activation(out=o.ap[:, :], in_=ps.ap[:, :], func=AF.Copy),
                       reads=[ps], writes=[o])
                trk.dma("sync", out=out_dram[tt * 128:(tt + 1) * 128, :], in_=o.ap[:, :], reads=[o])
            trk.barrier()

    def make_headrms_epi(self, es0, tag, psR):
        trk = self.trk
        sq = [Buf(self.sb(es0, tag + "_sq%d" % i, [128, 512], BF16)) for i in range(2)]
        rs = [Buf(self.sb(es0, tag + "_rs%d" % i, [128, 512], F32)) for i in range(2)]
        ob = [Buf(self.sb(es0, tag + "_ob%d" % i, [128, 512], BF16)) for i in range(3)]
        st = dict(i=0)

        def f(ps, n, gcol, gbuf, dst):
            i = st["i"]
            st["i"] += 1
            s_, r_, o_ = sq[i % 2], rs[i % 2], ob[i % 3]
            p2 = psR[i % len(psR)]
            trk.op("scalar", lambda h: h.activation(out=s_.ap[:, 0:n], in_=ps.ap[:, 0:n], func=AF.Square),
                   reads=[ps], writes=[s_])
            trk.op("tensor", lambda h: h.matmul(p2.ap[:, 0:n], lhsT=self.cH.ap[:, :], rhs=s_.ap[:, 0:n],
                                                start=True, stop=True), reads=[s_, self.cH], writes=[p2])
            trk.op("scalar", lambda h: h.activation(out=r_.ap[:, 0:n], in_=p2.ap[:, 0:n], func=AF.Ln,
                                                    bias=self.epsc.ap[:, 0:1], scale=1.0),
                   reads=[p2, self.epsc], writes=[r_])
            trk.op("scalar", lambda h: h.activation(out=r_.ap[:, 0:n], in_=r_.ap[:, 0:n], func=AF.Exp, scale=-0.5),
                   reads=[r_], writes=[r_])
            trk.op("vector", lambda h: h.scalar_tensor_tensor(out=o_.ap[:, 0:n], in0=ps.ap[:, 0:n], scalar=gcol,
                                                              in1=r_.ap[:, 0:n], op0=ALU.mult, op1=ALU.mult),
                   reads=[ps, r_, gbuf], writes=[o_])
            trk.dma("sync", out=dst, in_=o_.ap[:, 0:n], reads=[o_])
        return f

    def make_copy_epi(self, es0, tag, dt=BF16, func=AF.Copy, nbuf=3):
        trk = self.trk
        ob = [Buf(self.sb(es0, tag + "_cb%d" % i, [128, 512], dt)) for i in range(nbuf)]
        st = dict(i=0)

        def f(ps, m, n, dst):
            o_ = ob[st["i"] % nbuf]
            st["i"] += 1
            trk.op("scalar", lambda h: h.activation(out=o_.ap[0:m, 0:n], in_=ps.ap[0:m, 0:n], func=func),
                   reads=[ps], writes=[o_])
            trk.dma("sync", out=dst, in_=o_.ap[0:m, 0:n], reads=[o_])
        return f

    def make_resid_epi(self, es0, tag, xin, xout):
        trk = self.trk
        xb = [Buf(self.sb(es0, tag + "_xb%d" % i, [128, 512], F32)) for i in range(3)]
        st = dict(i=0)

        def f(es, c0, m, tk, t0, n, ps):
            b = xb[st["i"] % 3]
            st["i"] += 1
            trk.dma("sync", out=b.ap[:, 0:n], in_=xin[c0:c0 + 128, t0:t0 + n], writes=[b])
            trk.op("vector", lambda h: h.tensor_tensor(out=b.ap[:, 0:n], in0=ps.ap[:, 0:n], in1=b.ap[:, 0:n],
                                                       op=ALU.add), reads=[ps, b], writes=[b])
            trk.dma("sync", out=xout[c0:c0 + 128, t0:t0 + n], in_=b.ap[:, 0:n], reads=[b])
        return f


    SK = {"slc": (0, 1, 4480, 4736, 2049), "win": (1, 1, 1408, 1536, 2049), "cmp": (0, 16, 4096, 6144, 497),
          "d0": (2, 1, 1024, 1152, 2049), "d1": (3, 1, 1024, 1152, 2049), "d2": (4, 1, 1024, 1152, 2049)}

    def bankgen(self, rel, oh, ps):
        nc, trk = self.nc, self.trk
        self.Gd = self.dscr("Gd", [5, 16, GLEN], F32)
        self.skew = {k: self.dscr("sk_" + k, [16, 129 * v[3]], F32) for k, v in self.SK.items()}
        with contextlib.ExitStack() as es:
            rb = Buf(self.sb(es, "bg_rb", [32, 16], F32))
            eb = Buf(self.sb(es, "bg_eb", [32, 16], F32))
            trk.dma("sync", out=rb.ap[:, :], in_=rel, writes=[rb])
            trk.op("scalar", lambda h: h.activation(out=eb.ap[:, :], in_=rb.ap[:, :], func=AF.Exp), reads=[rb], writes=[eb])
            oht = [Buf(self.sb(es, "bg_oh%d" % i, [32, 512], F32)) for i in range(6)]
            gt = [Buf(self.sb(es, "bg_gt%d" % i, [16, 512], F32)) for i in range(6)]
            chunks = [(ty, c, min(512, GLEN - c)) for ty in range(5) for c in range(0, GLEN, 512)]

            def gload(k):
                ty, c, n = chunks[k]
                trk.dma("sync", out=oht[k % 6].ap[:, 0:n], in_=oh[ty, :, c:c + n], writes=[oht[k % 6]])
            for k in range(min(5, len(chunks))):
                gload(k)
            for k, (ty, c, n) in enumerate(chunks):
                o_, g_, p_ = oht[k % 6], gt[k % 6], ps[k % 6]
                trk.op("tensor", lambda h: h.matmul(p_.ap[0:16, 0:n], lhsT=eb.ap[:, :], rhs=o_.ap[:, 0:n],
                                                    start=True, stop=True), reads=[eb, o_], writes=[p_])
                trk.op("vector", lambda h: h.tensor_copy(out=g_.ap[:, 0:n], in_=p_.ap[0:16, 0:n]), reads=[p_], writes=[g_])
                trk.dma("sync", out=self.Gd[ty, :, c:c + n], in_=g_.ap[:, 0:n], reads=[g_])
                if k + 5 < len(chunks):
                    gload(k + 5)
            trk.barrier()
            dummy = Buf(None)
            dummy.bg = True
            self.bgwork = []
            for name, (ty, s_, Lr, P, g0) in self.SK.items():
                Lw = Lr + 127 * s_
                for h in range(16):
                    dst = bass.AP(self.skew[name].tensor, h * 129 * P, [[P + s_, 128], [1, Lw]])
                    src = bass.AP(self.Gd.tensor, (ty * 16 + h) * GLEN + g0, [[0, 128], [1, Lw]])
                    self.bgwork.append(lambda dst=dst, src=src: trk.dma("sync", out=dst, in_=src, writes=[dummy]))

    def bank_load(self, name, h, buf):
        ty, s_, Lr, P, g0 = self.SK[name]
        src = bass.AP(self.skew[name].tensor, h * 129 * P + 127 * s_, [[P, 128], [1, Lr]])
        self.trk.dma("sync", out=buf.ap[:, 0:Lr], in_=src, writes=[buf])

    def attn_core(self, W, QT, qap, nq, kv, num, den, consume=None):
        trk = self.trk
        sS, pf, pb = W["sS"], W["pf"], W["pb"]
        n = len(kv)
        st = W["st"]

        def emitS(i):
            t = kv[i]
            s_ = sS[(st["s"] + i) % len(sS)]
            mk = t.get("mask")
            trk.op("tensor", lambda h: h.matmul(s_.ap[:, 0:nq], lhsT=t["KT"], rhs=qap, start=True, stop=(mk is None)),
                   reads=[t["kb"], QT], writes=[s_], inc=(mk is None))
            if mk is not None:
                trk.op("tensor", lambda h: h.matmul(s_.ap[:, 0:nq], lhsT=mk[0], rhs=mk[1], start=False, stop=True),
                       reads=mk[2], writes=[s_])
        LA = len(sS) - 1
        for i in range(min(LA, n)):
            emitS(i)
        for i in range(n):
            if i + LA < n:
                emitS(i + LA)
            t = kv[i]
            s_ = sS[(st["s"] + i) % len(sS)]
            f_ = pf[(st["p"] + i) % len(pf)]
            b_ = pb[(st["p"] + i) % len(pb)]
            trk.op("scalar", lambda h: h.activation(out=f_.ap[:, 0:nq], in_=s_.ap[:, 0:nq], func=AF.Exp, scale=SCALE),
                   reads=[s_], writes=[f_])
            trk.op("vector", lambda h: h.tensor_tensor(out=b_.ap[:, 0:nq], in0=f_.ap[:, 0:nq], in1=t["E"], op=ALU.mult),
                   reads=[f_, t["eb"]], writes=[b_])
            if consume is not None:
                consume(i, b_)
            else:
                trk.op("tensor", lambda h: h.matmul(num.ap[:, 0:nq], lhsT=t["V"], rhs=b_.ap[:, 0:nq], start=(i == 0),
                                                    stop=(i == n - 1)), reads=[t["vb"], b_], writes=[num], inc=False)
                trk.op("tensor", lambda h: h.matmul(den.ap[:, 0:nq], lhsT=self.ones.ap[:, :], rhs=b_.ap[:, 0:nq],
                                                    start=(i == 0), stop=(i == n - 1)), reads=[self.ones, b_],
                       writes=[den, num])
        st["s"] += n
        st["p"] += n

    def run_pipe(self, W, tasks, defer=3):
        trk = self.trk
        sS, pf, pb = W["sS"], W["pf"], W["pb"]
        n = len(tasks)
        LA = len(sS) - 1

        def emitS(i):
            t = tasks[i]
            s_ = sS[i % len(sS)]
            mk = t.get("mask")
            nq = t["nq"]
            trk.op("tensor", lambda h: h.matmul(s_.ap[:, 0:nq], lhsT=t["KT"], rhs=t["qap"], start=True, stop=(mk is None)),
                   reads=[t["kb"], t["QT"]], writes=[s_], inc=(mk is None))
            if mk is not None:
                trk.op("tensor", lambda h: h.matmul(s_.ap[:, 0:nq], lhsT=mk[0], rhs=mk[1], start=False, stop=True),
                       reads=mk[2], writes=[s_])
        posts = []
        for i in range(min(LA, n)):
            emitS(i)
        for i in range(n):
            t = tasks[i]
            if t.get("hook"):
                t["hook"]()
            while posts and posts[0][0] <= i:
                posts.pop(0)[1]()
            if t["first"]:
                idx = [k for k, p in enumerate(posts) if any(b is t["num"] or b is t["den"] for b in p[2])]
                if idx:
                    for _ in range(idx[-1] + 1):
                        posts.pop(0)[1]()
            if i + LA < n:
                emitS(i + LA)
            nq = t["nq"]
            s_ = sS[i % len(sS)]
            f_ = pf[i % len(pf)]
            b_ = pb[i % len(pb)]
            num, den = t["num"], t["den"]
            numap = t.get("numap", None)
            denap = t.get("denap", None)
            if numap is None:
                numap = num.ap[:, 0:nq]
            if denap is None:
                denap = den.ap[:, 0:nq]
            trk.op("scalar", lambda h: h.activation(out=f_.ap[:, 0:nq], in_=s_.ap[:, 0:nq], func=AF.Exp, scale=SCALE),
                   reads=[s_], writes=[f_])
            trk.op("vector", lambda h: h.tensor_tensor(out=b_.ap[:, 0:nq], in0=f_.ap[:, 0:nq], in1=t["E"], op=ALU.mult),
                   reads=[f_, t["eb"]], writes=[b_])
            trk.op("tensor", lambda h: h.matmul(numap, lhsT=t["V"], rhs=b_.ap[:, 0:nq], start=t["first"],
                                                stop=t["last"]), reads=[t["vb"], b_], writes=[num], inc=False)
            dstart = t["first"] and (num is not den)
            trk.op("tensor", lambda h: h.matmul(denap, lhsT=self.ones.ap[:, :], rhs=b_.ap[:, 0:nq],
                                                start=dstart, stop=t["last"]), reads=[self.ones, b_],
                   writes=[den, num])
            if t["last"] and t.get("post"):
                pp = t["post"]
                if callable(pp):
                    posts.append((i + defer, pp, [num, den]))
                else:
                    for dly, fn, which in pp:
                        posts.append((i + dly, fn, [den] if which == "den" else ([num] if which == "num" else [num, den])))
                    posts.sort(key=lambda x: x[0])
        for p in posts:
            p[1]()

    def fin_gate(self, F, gate_src, nq):
        g_ = F["g"][F["gi"] % len(F["g"])]
        F["gi"] += 1
        self.trk.dma("sync", out=g_.ap[:, 0:nq], in_=gate_src, writes=[g_])
        return g_

    def fin_a(self, F, den, nq, g_):
        trk = self.trk
        r_ = F["r"][F["i"] % len(F["r"])]
        F["i"] += 1
        trk.op("scalar", lambda h: h.activation(out=r_.ap[:, 0:nq], in_=den.ap[:, 0:nq], func=AF.Ln,
                                                bias=self.tiny.ap[:, 0:1], scale=1.0), reads=[den, self.tiny], writes=[r_])
        trk.op("scalar", lambda h: h.activation(out=r_.ap[:, 0:nq], in_=r_.ap[:, 0:nq], func=AF.Exp, scale=-1.0),
               reads=[r_], writes=[r_])
        return r_

    def fin_b(self, F, acc, num, nq, r_, first, g_, out=None):
        trk = self.trk
        trk.op("vector", lambda h: h.tensor_tensor(out=r_.ap[:, 0:nq], in0=r_.ap[:, 0:nq], in1=g_.ap[:, 0:nq], op=ALU.mult),
               reads=[r_, g_], writes=[r_])
        if first:
            trk.op("vector", lambda h: h.tensor_tensor(out=acc.ap[:, 0:nq], in0=num.ap[:, 0:nq], in1=r_.ap[:, 0:nq],
                                                       op=ALU.mult), reads=[num, r_], writes=[acc])
        else:
            t_ = F["t"][F["ti"] % len(F["t"])]
            F["ti"] += 1
            trk.op("vector", lambda h: h.tensor_tensor(out=t_.ap[:, 0:nq], in0=num.ap[:, 0:nq], in1=r_.ap[:, 0:nq],
                                                       op=ALU.mult), reads=[num, r_], writes=[t_])
            dst = acc if out is None else out
            trk.op("vector", lambda h: h.tensor_tensor(out=dst.ap[:, 0:nq], in0=acc.ap[:, 0:nq], in1=t_.ap[:, 0:nq],
                                                       op=ALU.add), reads=[acc, t_], writes=[dst])

    def attn_work(self, es, tag):
        W = dict(st=dict(s=0, p=0))
        W["pf"] = [Buf(self.sb(es, tag + "_pf%d" % i, [128, 512], F32)) for i in range(4)]
        W["pb"] = [Buf(self.sb(es, tag + "_pb%d" % i, [128, 512], BF16)) for i in range(4)]
        return W

    def phase_compress(self, kvraw, cmp_w1, cmp_w2, cmp_pos, gq, kcmpT, vcmp, ps):
        trk = self.trk
        with contextlib.ExitStack() as es:
            w1 = Buf(self.sb(es, "c_w1", [128, 32, 128], BF16))
            w2 = Buf(self.sb(es, "c_w2", [128, 128], BF16))
            pos = Buf(self.sb(es, "c_pos", [128, 32], BF16))
            bias = Buf(self.sb(es, "c_bias", [128, 1], F32))
            raw = [Buf(self.sb(es, "c_raw%d" % i, [128, S], BF16)) for i in range(2)]
            gl = Buf(self.sb(es, "c_gl", [128, 256], BF16))
            sq = Buf(self.sb(es, "c_sq", [128, 256], BF16))
            rs = Buf(self.sb(es, "c_rs", [128, 256], F32))
            ko = Buf(self.sb(es, "c_ko", [128, 256], BF16))
            vo = Buf(self.sb(es, "c_vo", [128, 2, 128], BF16))
            trk.op("vector", lambda h: h.memset(gl.ap[:, :], 0.0), writes=[gl])
            trk.op("vector", lambda h: h.memset(ko.ap[:, :], 0.0), writes=[ko])
            k = 0
            for kv in range(2):
                trk.dma("gpsimd", out=w1.ap[:, :, :], in_=cmp_w1[kv].rearrange("(l p) n -> p l n", p=128), writes=[w1])
                trk.dma("gpsimd", out=w2.ap[:, :], in_=cmp_w2[kv], writes=[w2])
                trk.dma("gpsimd", out=pos.ap[:, :], in_=cmp_pos[kv], writes=[pos])
                pb_ = ps[0]
                for l in range(32):
                    trk.op("tensor", lambda h: h.matmul(pb_.ap[:, 0:1], lhsT=w1.ap[:, l, :], rhs=pos.ap[:, l:l + 1],
                                                        start=(l == 0), stop=(l == 31)), reads=[w1, pos], writes=[pb_],
                           inc=(l == 31))
                trk.op("vector", lambda h: h.tensor_copy(out=bias.ap[:, :], in_=pb_.ap[:, 0:1]), reads=[pb_], writes=[bias])
                for g in range(4):
                    r_ = raw[k % 2]
                    k += 1
                    trk.dma("sync", out=r_.ap[:, :], in_=kvraw[kv * 4 + g], writes=[r_])
                    p1 = ps[1]
                    for l in range(32):
                        trk.op("tensor", lambda h: h.matmul(p1.ap[:, 0:255], lhsT=w1.ap[:, l, :],
                                                            rhs=ap3(r_.ap[:, l:l + 1], [[16, 255]]), start=(l == 0),
                                                            stop=(l == 31)), reads=[w1, r_], writes=[p1], inc=(l == 31))
                    trk.op("scalar", lambda h: h.activation(out=gl.ap[:, 0:255], in_=p1.ap[:, 0:255],
                                                            func=AF.Gelu_apprx_tanh, bias=bias.ap[:, 0:1], scale=1.0),
                           reads=[p1, bias], writes=[gl])
                    if kv == 0:
                        p2 = ps[2]
                        trk.op("tensor", lambda h: h.matmul(p2.ap[:, 0:255], lhsT=w2.ap[:, :], rhs=gl.ap[:, 0:255],
                                                            start=True, stop=True), reads=[w2, gl], writes=[p2])
                        trk.op("scalar", lambda h: h.activation(out=sq.ap[:, 0:255], in_=p2.ap[:, 0:255], func=AF.Square),
                               reads=[p2], writes=[sq])
                        p3 = ps[3]
                        trk.op("tensor", lambda h: h.matmul(p3.ap[:, 0:255], lhsT=self.cH.ap[:, :], rhs=sq.ap[:, 0:255],
                                                            start=True, stop=True), reads=[self.cH, sq], writes=[p3])
                        trk.op("scalar", lambda h: h.activation(out=rs.ap[:, 0:255], in_=p3.ap[:, 0:255], func=AF.Sqrt,
                                                                bias=self.epsc.ap[:, 0:1], scale=1.0),
                               reads=[p3, self.epsc], writes=[rs])
                        trk.op("vector", lambda h: h.reciprocal(out=rs.ap[:, 0:255], in_=rs.ap[:, 0:255]), reads=[rs], writes=[rs])
                        trk.op("vector", lambda h: h.scalar_tensor_tensor(out=ko.ap[:, 0:255], in0=p2.ap[:, 0:255],
                                                                          scalar=gq.ap[:, 1:2], in1=rs.ap[:, 0:255],
                                                                          op0=ALU.mult, op1=ALU.mult),
                               reads=[p2, rs, gq], writes=[ko])
                        trk.dma("sync", out=kcmpT[g], in_=ko.ap[:, :], reads=[ko])
                    else:
                        for nt in range(2):
                            p2 = ps[2 + nt]
                            trk.op("tensor", lambda h: h.matmul(p2.ap[:, 0:128], lhsT=gl.ap[:, nt * 128:(nt + 1) * 128],
                                                                rhs=w2.ap[:, :], start=True, stop=True),
                                   reads=[w2, gl], writes=[p2])
                            trk.op("vector", lambda h: h.tensor_copy(out=vo.ap[:, nt, :], in_=p2.ap[:, 0:128]),
                                   reads=[p2], writes=[vo])
                        trk.dma("sync", out=vcmp[g], in_=vo.ap[:, :, :], reads=[vo])
            trk.barrier()

    def finalize(self, F, num, den, nq, gate_src, first):
        trk = self.trk
        i = F["i"]
        F["i"] += 1
        r_ = F["r"][i % 2]
        trk.op("vector", lambda h: h.tensor_scalar(out=r_.ap[:, 0:nq], in0=den.ap[:, 0:nq], scalar1=1e-30, scalar2=None,
                                                   op0=ALU.max), reads=[den], writes=[r_])
        trk.op("scalar", lambda h: h.activation(out=r_.ap[:, 0:nq], in_=r_.ap[:, 0:nq], func=AF.Ln), reads=[r_], writes=[r_])
        trk.op("scalar", lambda h: h.activation(out=r_.ap[:, 0:nq], in_=r_.ap[:, 0:nq], func=AF.Exp, scale=-1.0),
               reads=[r_], writes=[r_])
        if gate_src is not None:
            g_ = F["g"][i % 2]
            trk.dma("sync", out=g_.ap[:, 0:nq], in_=gate_src, writes=[g_])
            trk.op("vector", lambda h: h.tensor_tensor(out=r_.ap[:, 0:nq], in0=r_.ap[:, 0:nq], in1=g_.ap[:, 0:nq],
                                                       op=ALU.mult), reads=[r_, g_], writes=[r_])
        acc = F["acc"]
        if first:
            trk.op("vector", lambda h: h.tensor_tensor(out=acc.ap[:, 0:nq], in0=num.ap[:, 0:nq], in1=r_.ap[:, 0:nq],
                                                       op=ALU.mult), reads=[num, r_], writes=[acc])
        else:
            t_ = F["t"]
            trk.op("vector", lambda h: h.tensor_tensor(out=t_.ap[:, 0:nq], in0=num.ap[:, 0:nq], in1=r_.ap[:, 0:nq],
                                                       op=ALU.mult), reads=[num, r_], writes=[t_])
            trk.op("gpsimd", lambda h: h.tensor_tensor(out=acc.ap[:, 0:nq], in0=acc.ap[:, 0:nq], in1=t_.ap[:, 0:nq],
                                                       op=ALU.add), reads=[acc, t_], writes=[acc])

    def phase_attn0(self, qT0, kT_slc, kT_win, v_slc, v_win, kcmpT, vcmp, gatesT, oT0, selc, expand, ovl, cst, ps):
        trk = self.trk
        with contextlib.ExitStack() as es:
            ksl = Buf(self.sb(es, "a_ksl", [128, S], BF16))
            kwi = Buf(self.sb(es, "a_kwi", [128, S], BF16))
            vsl = Buf(self.sb(es, "a_vsl", [128, 32, 128], BF16))
            vwi = Buf(self.sb(es, "a_vwi", [128, 32, 128], BF16))
            kcm = Buf(self.sb(es, "a_kcm", [128, 256], BF16))
            vcm = Buf(self.sb(es, "a_vcm", [128, 2, 128], BF16))
            QT = [Buf(self.sb(es, "a_qt%d" % i, [128, S], BF16)) for i in range(2)]
            bss = [Buf(self.sb(es, "a_bs%d" % i, [128, 4480], F32)) for i in range(2)]
            bws = [Buf(self.sb(es, "a_bw%d" % i, [128, 1408], F32)) for i in range(2)]
            bcs = [Buf(self.sb(es, "a_bc%d" % i, [128, 4096], F32)) for i in range(2)]
            bc = bcs[0]
            hn = 0
            negT = Buf(self.sb(es, "a_negT", [128, S], BF16))
            trk.op("vector", lambda h: h.memset(negT.ap[:, :], 0.0), writes=[negT])
            negm = Buf(self.sb(es, "a_negm", [128, 32, 64], F32))
            score = Buf(self.sb(es, "a_score", [128, 32, 64], F32))
            expd = Buf(self.sb(es, "a_expd", [128, 32, 128], BF16))
            ov = Buf(self.sb(es, "a_ov", [128, 2, 65], BF16))
            ident = Buf(self.sb(es, "a_id", [128, 128], F32))
            m8a = Buf(self.sb(es, "a_m8a", [128, 8], F32))
            m8b = Buf(self.sb(es, "a_m8b", [128, 8], F32))
            sc2 = Buf(self.sb(es, "a_sc2", [128, 64], F32))
            recs = [Buf(self.sb(es, "a_rec%d" % i, [128, 4], F32)) for i in range(4)]
            imptmp = [Buf(self.sb(es, "a_imp%d" % i, [128, 4, 64], F32)) for i in range(2)]
            pi = 0
            W = self.attn_work(es, "a")
            W["sS"] = [ps[0], ps[1], ps[2]]
            F = dict(i=0, gi=0, ti=0, r=[Buf(self.sb(es, "a_r%d" % i, [128, 512], F32)) for i in range(3)],
                     g=[Buf(self.sb(es, "a_g%d" % i, [128, 512], F32)) for i in range(6)],
                     t=[Buf(self.sb(es, "a_t%d" % i, [128, 512], F32)) for i in range(2)], acc=None)
            accs = [Buf(self.sb(es, "a_acc%d" % i, [128, 512], F32)) for i in range(2)]
            obs = [Buf(self.sb(es, "a_ob%d" % i, [128, 512], BF16)) for i in range(2)]
            trk.dma("gpsimd", out=expd.ap[:, :, :], in_=expand.rearrange("p (k c) -> p k c", c=128), writes=[expd])
            trk.dma("gpsimd", out=ov.ap[:, :, :], in_=ovl.rearrange("p (k c) -> p k c", c=65), writes=[ov])
            trk.dma("sync", out=ident.ap[:, :], in_=cst[:, 0:128], writes=[ident])
            psI, psT = ps[3], ps[3]
            nb = 0
            qn = 0
            for g in range(4):
                trk.dma("sync", out=ksl.ap[:, :], in_=kT_slc[g], writes=[ksl])
                trk.dma("sync", out=kwi.ap[:, :], in_=kT_win[g], writes=[kwi])
                trk.dma("sync", out=vsl.ap[:, :, :], in_=v_slc.rearrange("(k p) c -> p k c", p=128)[:, :, g * 128:(g + 1) * 128], writes=[vsl])
                trk.dma("sync", out=vwi.ap[:, :, :], in_=v_win.rearrange("(k p) c -> p k c", p=128)[:, :, g * 128:(g + 1) * 128], writes=[vwi])
                trk.dma("sync", out=kcm.ap[:, :], in_=kcmpT[g], writes=[kcm])
                trk.dma("sync", out=vcm.ap[:, :, :], in_=vcmp[g], writes=[vcm])
                trk.dma("sync", out=score.ap[:, :, :], in_=selc.rearrange("p (k c) -> p k c", c=64), writes=[score])

                def cmp_kv(qt, bc):
                    lst = []
                    for nt in range(2):
                        if nt == 1 and qt <= 3:
                            continue
                        j0 = 512 * qt - 2048 * nt
                        lst.append(dict(KT=kcm.ap[:, nt * 128:(nt + 1) * 128], kb=kcm, V=vcm.ap[:, nt, :], vb=vcm,
                                        E=bc.ap[:, j0:j0 + 512], eb=bc, nt=nt))
                    return lst
                def loadp1(h_, slot_):
                    trk.dma("sync", out=QT[slot_].ap[:, :], in_=qT0[h_], writes=[QT[slot_]])
                    self.bank_load("cmp", h_, bcs[slot_])
                loadp1(4 * g, hn % 2)
                for hl in range(4):
                    h = 4 * g + hl
                    slot = hn % 2
                    hn += 1
                    Q, bc_ = QT[slot], bcs[slot]
                    if hl < 3:
                        loadp1(h + 1, (slot + 1) % 2)
                    for qt in range(8):
                        kvl = cmp_kv(qt, bc_)
                        held = []
                        self.attn_core(W, Q, Q.ap[:, qt * 512:(qt + 1) * 512], 512, kvl, None, None,
                                       consume=lambda i, b_: held.append(b_))
                        psI = ps[3 + pi % 2]
                        rec = recs[pi % 4]
                        tmp = imptmp[pi % 2]
                        pi += 1
                        for sub in range(4):
                            for i, t in enumerate(kvl):
                                b_ = held[i]
                                lastmm = (i == len(kvl) - 1)
                                trk.op("tensor", lambda hh: hh.matmul(psI.ap[:, sub * 65:(sub + 1) * 65],
                                                                      lhsT=b_.ap[:, sub * 128:(sub + 1) * 128],
                                                                      rhs=ov.ap[:, t["nt"], :], start=(i == 0), stop=lastmm),
                                       reads=[b_, ov], writes=[psI], inc=(lastmm and sub == 3))
                        den4 = ap3(psI.ap[:, 64:65], [[65, 4]])
                        imp4 = ap3(psI.ap[:, 0:1], [[65, 4], [1, 64]])
                        recb = ap3(rec.ap[:, 0:1], [[1, 4], [0, 64]])
                        trk.op("vector", lambda hh: hh.tensor_scalar(out=rec.ap[:, 0:4], in0=den4, scalar1=1e-30, scalar2=None,
                                                                     op0=ALU.max), reads=[psI], writes=[rec])
                        trk.op("vector", lambda hh: hh.reciprocal(out=rec.ap[:, 0:4], in_=rec.ap[:, 0:4]), reads=[rec], writes=[rec])
                        trk.op("vector", lambda hh: hh.tensor_tensor(out=tmp.ap[:, :, :], in0=imp4, in1=recb, op=ALU.mult),
                               reads=[psI, rec], writes=[tmp])
                        trk.op("vector", lambda hh: hh.tensor_tensor(out=score.ap[:, qt * 4:(qt + 1) * 4, :],
                                                                     in0=score.ap[:, qt * 4:(qt + 1) * 4, :], in1=tmp.ap[:, :, :],
                                                                     op=ALU.add), reads=[score, tmp], writes=[score])
                def loadhead(h, slot):
                    trk.dma("sync", out=QT[slot].ap[:, :], in_=qT0[h], writes=[QT[slot]])
                    self.bank_load("cmp", h, bcs[slot])
                    self.bank_load("slc", h, bss[slot])
                    self.bank_load("win", h, bws[slot])
                loadhead(4 * g, hn % 2)
                for qi in range(32):
                    trk.op("vector", lambda hh: hh.max(out=m8a.ap[:, :], in_=score.ap[:, qi, :]), reads=[score], writes=[m8a])
                    trk.op("vector", lambda hh: hh.match_replace(out=sc2.ap[:, :], in_to_replace=m8a.ap[:, :],
                                                                 in_values=score.ap[:, qi, :], imm_value=-3e38),
                           reads=[score, m8a], writes=[sc2])
                    trk.op("vector", lambda hh: hh.max(out=m8b.ap[:, :], in_=sc2.ap[:, :]), reads=[sc2], writes=[m8b])
                    trk.op("vector", lambda hh: hh.tensor_scalar(out=negm.ap[:, qi, :], in0=score.ap[:, qi, :],
                                                                 scalar1=m8b.ap[:, 7:8], scalar2=1.0, op0=ALU.is_ge,
                                                                 op1=ALU.subtract), reads=[score, m8b], writes=[negm])
                    trk.op("tensor", lambda hh: hh.transpose(out=psT.ap[0:64, (qi % 4) * 128:(qi % 4 + 1) * 128],
                                                             in_=negm.ap[:, qi, :], identity=ident.ap[:, :]),
                           reads=[negm, ident], writes=[psT])
                    if qi % 4 == 3:
                        c0 = (qi // 4) * 512
                        trk.op("scalar", lambda hh: hh.activation(out=negT.ap[0:64, c0:c0 + 512], in_=psT.ap[0:64, :], func=AF.Copy),
                               reads=[psT], writes=[negT])
                tasks = []
                for hl in range(4):
                    h = 4 * g + hl
                    slot = hn % 2
                    hn += 1
                    Q, bc_, bs_, bw_ = QT[slot], bcs[slot], bss[slot], bws[slot]
                    firsttask = True
                    for qt in range(8):
                        qap = Q.ap[:, qt * 512:(qt + 1) * 512]
                        acc = accs[qt % 2]
                        branches = []
                        lst = []
                        for nt in range(2):
                            if nt == 1 and qt <= 3:
                                continue
                            j0 = 512 * qt - 2048 * nt
                            lst.append(dict(KT=kcm.ap[:, nt * 128:(nt + 1) * 128], kb=kcm, V=vcm.ap[:, nt, :], vb=vcm,
                                            E=bc_.ap[:, j0:j0 + 512], eb=bc_))
                        branches.append(lst)
                        lst = []
                        for kt in range(0, 4 * qt + 4):
                            j0 = 512 * qt - 128 * kt + 384
                            off = max(0, kt - 4 * qt) * 128
                            n_ = 512 - off
                            lst.append(dict(KT=ksl.ap[:, kt * 128:(kt + 1) * 128], kb=ksl, V=vsl.ap[:, kt, :], vb=vsl,
                                            E=bs_.ap[:, j0 + off:j0 + 512], eb=bs_, off=off, nq=n_,
                                            mask=(expd.ap[:, kt, :], negT.ap[:, qt * 512 + off:(qt + 1) * 512], [expd, negT])))
                        branches.append(lst)
                        lst = []
                        kts_w = [kt for kt in range(max(0, 4 * qt - 4), 4 * qt + 4)]
                        kts_w = [4 * qt] + [kt for kt in kts_w if kt != 4 * qt]
                        for kt in kts_w:
                            j0 = 512 * qt - 128 * kt + 384
                            d_ = kt - 4 * qt
                            off = max(0, d_) * 128
                            n_ = (512 - off) if d_ >= 0 else min(512, 128 * (5 + d_))
                            lst.append(dict(KT=kwi.ap[:, kt * 128:(kt + 1) * 128], kb=kwi, V=vwi.ap[:, kt, :], vb=vwi,
                                            E=bw_.ap[:, j0 + off:j0 + off + n_], eb=bw_, off=off, nq=n_))
                        branches.append(lst)
                        gst = {}

                        def ghook(h=h, qt=qt, gst=gst):
                            for j in range(3):
                                gsrc = bass.AP(gatesT.tensor, (h * 3 + j) * S + qt * 512, [[0, 128], [1, 512]])
                                gst[j] = self.fin_gate(F, gsrc, 512)
                        qfirst = True
                        for j, kvl in enumerate(branches):
                            num, den = ps[3 + nb % 3], ps[6 + nb % 2]
                            nb += 1
                            st = {}

                            def postA(h=h, j=j, qt=qt, den=den, st=st, gst=gst):
                                st["r"] = self.fin_a(F, den, 512, gst[j])

                            def postB(h=h, j=j, qt=qt, num=num, acc=acc, st=st, gst=gst):
                                o_ = obs[qt % 2]
                                self.fin_b(F, acc, num, 512, st["r"], j == 0, gst[j], out=(o_ if j == 2 else None))
                                if j == 2:
                                    trk.dma("sync", out=oT0[h, :, qt * 512:(qt + 1) * 512], in_=o_.ap[:, :], reads=[o_])
                            post = [(4, postA, "den"), (6, postB, "num")]
                            for ti, t in enumerate(kvl):
                                off = t.get("off", 0)
                                n_ = t.get("nq", 512)
                                t.update(QT=Q, qap=Q.ap[:, qt * 512 + off:qt * 512 + off + n_], nq=n_, num=num, den=den,
                                         numap=num.ap[:, off:off + n_], denap=den.ap[:, off:off + n_],
                                         first=(ti == 0), last=(ti == len(kvl) - 1))
                                if ti == len(kvl) - 1:
                                    t["post"] = post
                                hooks = []
                                if qfirst:
                                    qfirst = False
                                    hooks.append(ghook)
                                if firsttask:
                                    firsttask = False
                                    if hl < 3:
                                        hooks.append(lambda hh=h + 1, sl=(slot + 1) % 2: loadhead(hh, sl))
                                if hooks:
                                    t["hook"] = (lambda hs=hooks: [f() for f in hs])
                                tasks.append(t)
                self.run_pipe(W, tasks)
            trk.barrier()

    def phase_attn1(self, qT1, kT_sh, v_sh, oT1, ps):
        trk = self.trk
        with contextlib.ExitStack() as es:
            kT = Buf(self.sb(es, "b_kT", [128, S], BF16))
            Vd = [Buf(self.sb(es, "b_v%d" % i, [128, 32, 128], BF16)) for i in range(3)]
            QT = [Buf(self.sb(es, "b_qt%d" % i, [128, S], BF16)) for i in range(2)]
            bk = [Buf(self.sb(es, "b_bk%d" % i, [128, 1024], F32)) for i in range(2)]
            naccs = [Buf(self.sb(es, "b_nacc%d" % i, [128, S], F32)) for i in range(2)]
            daccs = [Buf(self.sb(es, "b_dacc%d" % i, [128, S], F32)) for i in range(2)]
            obs = [Buf(self.sb(es, "b_ob%d" % i, [128, S], BF16)) for i in range(2)]
            W = self.attn_work(es, "b")
            W["sS"] = [ps[0], ps[1], ps[2]]
            dils = (1, 4, 16)
            nb = 0
            cn = 0

            def loadcombo(h, gi, slot):
                trk.dma("sync", out=QT[slot].ap[:, :], in_=qT1[gi * 16 + h], writes=[QT[slot]])
                self.bank_load("d%d" % gi, h, bk[slot])
            for g in range(4):
                trk.dma("sync", out=kT.ap[:, :], in_=kT_sh[g], writes=[kT])
                for gi, dil in enumerate(dils):
                    ntl = S // dil // 128
                    for res in range(dil):
                        src = bass.AP(v_sh.tensor, res * 512 + g * 128, [[dil * 512, 128], [dil * 128 * 512, ntl], [1, 128]])
                        trk.dma("sync", out=Vd[gi].ap[:, res * ntl:(res + 1) * ntl, :], in_=src, writes=[Vd[gi]])
                tasks = []
                combos = [(4 * g + hl, gi) for hl in range(4) for gi in range(3)]
                loadcombo(combos[0][0], combos[0][1], cn % 2)
                for ci, (h, gi) in enumerate(combos):
                    dil = dils[gi]
                    slot = cn % 2
                    cn += 1
                    Q, B = QT[slot], bk[slot]
                    nacc, dacc, ob = naccs[h % 2], daccs[h % 2], obs[h % 2]
                    L = S // dil
                    ntl = L // 128
                    nq = min(512, L)
                    firsttask = True
                    segs = [(res, a0) for res in range(dil) for a0 in range(0, L, nq)]
                    for si, (res, a0) in enumerate(segs):
                        qap = ap3(Q.ap[:, res + dil * a0:res + dil * a0 + 1], [[dil, nq]])
                        if nq == 256:
                            num = den = ps[3 + nb % 5]
                            numap, denap = num.ap[:, 0:256], num.ap[:, 256:512]
                        else:
                            num, den = ps[3 + nb % 3], ps[6 + nb % 2]
                            numap, denap = num.ap[:, 0:nq], den.ap[:, 0:nq]
                        nb += 1
                        kts = list(range(max(0, a0 // 128 - 1), (a0 + nq) // 128))
                        lastseg = (si == len(segs) - 1)

                        def post(h=h, gi=gi, res=res, a0=a0, dil=dil, nq=nq, num=num, den=den, nacc=nacc, dacc=dacc, ob=ob,
                                 lastseg=lastseg, numap=numap, denap=denap):
                            nv = ap3(nacc.ap[:, res + dil * a0:res + dil * a0 + 1], [[dil, nq]])
                            dv = ap3(dacc.ap[:, res + dil * a0:res + dil * a0 + 1], [[dil, nq]])
                            if gi == 0:
                                trk.op("vector", lambda hh: hh.tensor_copy(out=nv, in_=numap), reads=[num], writes=[nacc])
                                trk.op("scalar", lambda hh: hh.activation(out=dv, in_=denap, func=AF.Copy),
                                       reads=[den], writes=[dacc])
                            else:
                                trk.op("vector", lambda hh: hh.tensor_tensor(out=nv, in0=numap, in1=nv, op=ALU.add),
                                       reads=[num, nacc], writes=[nacc])
                                trk.op("vector", lambda hh: hh.tensor_tensor(out=dv, in0=denap, in1=dv, op=ALU.add),
                                       reads=[den, dacc], writes=[dacc])
                            if gi == 2 and lastseg:
                                trk.op("scalar", lambda hh: hh.activation(out=dacc.ap[:, :], in_=dacc.ap[:, :], func=AF.Ln),
                                       reads=[dacc], writes=[dacc])
                                trk.op("scalar", lambda hh: hh.activation(out=dacc.ap[:, :], in_=dacc.ap[:, :], func=AF.Exp,
                                                                          scale=-1.0), reads=[dacc], writes=[dacc])
                                trk.op("vector", lambda hh: hh.tensor_tensor(out=ob.ap[:, :], in0=nacc.ap[:, :], in1=dacc.ap[:, :],
                                                                             op=ALU.mult), reads=[nacc, dacc], writes=[ob])
                                trk.dma("sync", out=oT1[h], in_=ob.ap[:, :], reads=[ob])
                        for ti, kt in enumerate(kts):
                            j0 = a0 - 128 * kt + 384
                            d_ = kt - a0 // 128
                            off = max(0, 128 * d_)
                            n_ = min(nq, 128 * d_ + 256) - off
                            qa = ap3(Q.ap[:, res + dil * (a0 + off):res + dil * (a0 + off) + 1], [[dil, n_]])
                            nbase = 0
                            dbase = 256 if (num is den) else 0
                            t = dict(KT=ap3(kT.ap[:, res + dil * 128 * kt:res + dil * 128 * kt + 1], [[dil, 128]]), kb=kT,
                                     V=Vd[gi].ap[:, res * ntl + kt, :], vb=Vd[gi], E=B.ap[:, j0 + off:j0 + off + n_], eb=B,
                                     QT=Q, qap=qa, nq=n_, num=num, den=den,
                                     numap=num.ap[:, nbase + off:nbase + off + n_], denap=den.ap[:, dbase + off:dbase + off + n_],
                                     first=(ti == 0), last=(ti == len(kts) - 1))
                            if ti == len(kts) - 1:
                                t["post"] = post
                            if firsttask:
                                firsttask = False
                                if ci + 1 < len(combos):
                                    t["hook"] = (lambda c=combos[ci + 1], sl=(slot + 1) % 2: loadcombo(c[0], c[1], sl))
                            tasks.append(t)
                self.run_pipe(W, tasks)
            trk.barrier()

    def load_hT(self, src, hT, hparts):
        v = src.rearrange("h p t -> p h t")
        for i in range(16):
            self.trk.dma("sync", out=hT[:, :, i * 256:(i + 1) * 256], in_=v[:, :, i * 256:(i + 1) * 256], writes=[hparts[i]])

    def ffn(self, li, xin, xout, gname, w_up, cwd, w_dn, actT, ps):
        trk = self.trk
        with contextlib.ExitStack() as e1:
            hT = self.sb(e1, "f_hT", [128, 16, S + 2], BF16)
            hparts = [Buf(None) for _ in range(16)]
            z = Buf(None)
            trk.op("vector", lambda h: h.memset(hT[:, :, 0:2], 0.0), writes=[hparts[0]])
            self.phase_norm(xin, gname, hT, hparts, 2, ps[6:8])
            with contextlib.ExitStack() as e2:
                cw = Buf(self.sb(e2, "f_cw", [128, 88, 3], F32))
                trk.dma("sync", out=cw.ap[:, :, :], in_=cwd, writes=[cw])
                tg = [Buf(self.sb(e2, "f_tg%d" % i, [128, 456], F32)) for i in range(9)]
                tv = [Buf(self.sb(e2, "f_tv%d" % i, [128, 456], F32)) for i in range(2)]
                sg = [Buf(self.sb(e2, "f_sg%d" % i, [128, 456], F32)) for i in range(2)]
                ob = [Buf(self.sb(e2, "f_ob%d" % i, [128, 456], BF16)) for i in range(2)]
                toks = [(456 * i, min(458, S + 2 - 456 * i)) for i in range(9)]

                def epi(es_, c, m, tk, t0, n, p):
                    jb = c // 128
                    no = n - 2
                    isg = c < DFF
                    t_ = tg[tk] if isg else tv[tk % 2]
                    trk.op("scalar", lambda h: h.activation(out=t_.ap[:, 0:no], in_=p.ap[:, 2:n], func=AF.Copy,
                                                            scale=cw.ap[:, jb, 2:3]), reads=[p, cw], writes=[t_])
                    trk.op("vector", lambda h: h.scalar_tensor_tensor(out=t_.ap[:, 0:no], in0=p.ap[:, 1:n - 1],
                                                                      scalar=cw.ap[:, jb, 1:2], in1=t_.ap[:, 0:no],
                                                                      op0=ALU.mult, op1=ALU.add), reads=[p, cw, t_], writes=[t_])
                    trk.op("vector", lambda h: h.scalar_tensor_tensor(out=t_.ap[:, 0:no], in0=p.ap[:, 0:n - 2],
                                                                      scalar=cw.ap[:, jb, 0:1], in1=t_.ap[:, 0:no],
                                                                      op0=ALU.mult, op1=ALU.add), reads=[p, cw, t_], writes=[t_])
                    if not isg:
                        s_, o_, g_ = sg[tk % 2], ob[tk % 2], tg[tk]
                        trk.op("scalar", lambda h: h.activation(out=s_.ap[:, 0:no], in_=g_.ap[:, 0:no], func=AF.Silu),
                               reads=[g_], writes=[s_])
                        trk.op("gpsimd", lambda h: h.tensor_tensor(out=o_.ap[:, 0:no], in0=s_.ap[:, 0:no], in1=t_.ap[:, 0:no],
                                                                   op=ALU.mult), reads=[s_, t_], writes=[o_])
                        trk.dma("sync", out=actT[jb - 44, :, t0:t0 + no], in_=o_.ap[:, 0:no], reads=[o_])
                cols = [[(j * 128, 128), (DFF + j * 128, 128)] for j in range(44)]
                self.gemm_fm(hT, hparts, 2, 16, w_up, cols, toks, epi, ps[0:4], 256, "fu%d" % li)
        with contextlib.ExitStack() as e1:
            wbd = [Buf(self.sb(e1, "fd_w%d" % i, [128, 44, 512], BF16)) for i in range(2)]
            at = [Buf(self.sb(e1, "fd_a%d" % i, [128, 44, 512], BF16)) for i in range(2)]
            rsd = self.make_resid_epi(e1, "fd%d" % li, xin, xout)
            av = actT.rearrange("k p t -> p k t")
            Wv = w_dn.rearrange("(kc p) n -> p kc n", p=128)

            def loadw(ng):
                for k0 in range(0, 44, 11):
                    trk.dma("gpsimd", out=wbd[ng % 2].ap[:, k0:k0 + 11, :], in_=Wv[:, k0:k0 + 11, ng * 512:(ng + 1) * 512],
                            writes=[wbd[ng % 2]])

            def loada(i):
                tg = i % 8
                for k0 in range(0, 44, 11):
                    trk.dma("sync", out=at[i % 2].ap[:, k0:k0 + 11, :], in_=av[:, k0:k0 + 11, tg * 512:(tg + 1) * 512],
                            writes=[at[i % 2]])
            loadw(0)
            loada(0)
            k = 0
            pend = []
            for ng in range(4):
                if ng + 1 < 4:
                    loadw(ng + 1)
                for tg in range(8):
                    i = ng * 8 + tg
                    if i + 1 < 32:
                        loada(i + 1)
                    a_, w_ = at[i % 2], wbd[ng % 2]
                    for nb in range(4):
                        p = ps[k % 4]
                        k += 1
                        for kc in range(44):
                            trk.op("tensor", lambda h: h.matmul(p.ap[:, 0:512], lhsT=w_.ap[:, kc, nb * 128:(nb + 1) * 128],
                                                                rhs=a_.ap[:, kc, :], start=(kc == 0), stop=(kc == 43)),
                                   reads=[w_, a_], writes=[p], inc=(kc == 43))
                        if pend:
                            pend.pop()()
                        pend.append(lambda a=(None, ng * 512 + nb * 128, 128, 0, tg * 512, 512, p): rsd(*a))
            if pend:
                pend.pop()()
            trk.barrier()

    def build(self, upto=99):
        nc = self.nc
        din, dscr = self.din, self.dscr
        xT = din("xT", [D, S])
        rel = din("rel_bias", [32, 16])
        an = [din("attn_norm%d" % i, [128, 16]) for i in range(2)]
        fn = [din("ffn_norm%d" % i, [128, 16]) for i in range(2)]
        w_in = din("a_w_in", [D, A_IN])
        gq0 = din("gq0", [128, 4])
        cmp_pos = din("cmp_posT", [2, 128, 32])
        cmp_w1 = din("a_cmp_w1", [2, 4096, 128])
        cmp_w2 = din("a_cmp_w2", [2, 128, 128])
        a_w_out = din("a_w_out", [D, D])
        kvn = din("kv_norm", [128, 16])
        kv_w = din("kv_w", [D, 1024])
        gq1 = din("gq1", [128, 4])
        b_w_q = din("b_w_q", [D, 6144])
        b_w_out = din("b_w_out", [D, D])
        w_up = [din("ffn_w_up%d" % i, [D, 2 * DFF]) for i in range(2)]
        cw = [din("ffn_conv%d" % i, [128, 88, 3]) for i in range(2)]
        w_dn = [din("ffn_w_down%d" % i, [DFF, D]) for i in range(2)]
        oh = din("onehot", [5, 32, GLEN])
        cst = din("consts", [128, 1024])
        selc = din("selc", [128, 32 * 64])
        expand = din("expand", [128, 32 * 128])
        ovl = din("ovl", [128, 130])

        qT0 = dscr("qT0", [16, 128, S], BF16)
        kT_slc = dscr("kT_slc", [4, 128, S], BF16)
        kT_win = dscr("kT_win", [4, 128, S], BF16)
        kvraw = dscr("kvraw", [8, 128, S], BF16)
        v_slc = dscr("v_slc", [S, 512], BF16)
        v_win = dscr("v_win", [S, 512], BF16)
        gatesT = dscr("gatesT", [48, S], F32)
        oT0 = dscr("oT0", [16, 128, S], BF16)
        x1T = dscr("x1T", [D, S], F32)
        actT = dscr("actT", [44, 128, S], BF16)
        x2T = dscr("x2T", [D, S], F32)
        kcmpT = dscr("kcmpT", [4, 128, 256], BF16)
        vcmp = dscr("vcmp", [4, 128, 2, 128], BF16)
        kT_sh = dscr("kT_sh", [4, 128, S], BF16)
        v_sh = dscr("v_sh", [S, 512], BF16)
        qT1 = dscr("qT1", [48, 128, S], BF16)
        oT1 = dscr("oT1", [16, 128, S], BF16)
        x3T = dscr("x3T", [D, S], F32)
        outT = nc.dram_tensor("outT", [D, S], F32, kind="ExternalOutput").ap()

        with contextlib.ExitStack() as es:
            self.trk = trk = Trk(nc, es)
            ps = [Buf(es.enter_context(nc.psum_tensor("ps%d" % i, [128, 512], F32))) for i in range(8)]
            self.ones = Buf(self.sb(es, "ones", [128, 128], BF16))
            trk.op("vector", lambda h: h.memset(self.ones.ap[:, :], 1.0), writes=[self.ones])
            self.epsc = Buf(self.sb(es, "epsc", [128, 1], F32))
            trk.op("vector", lambda h: h.memset(self.epsc.ap[:, :], EPS), writes=[self.epsc])
            self.tiny = Buf(self.sb(es, "tiny", [128, 1], F32))
            trk.op("vector", lambda h: h.memset(self.tiny.ap[:, :], 1e-18), writes=[self.tiny])
            self.cD = Buf(self.sb(es, "cD", [128, 128], BF16))
            trk.op("vector", lambda h: h.memset(self.cD.ap[:, :], 1.0 / D), writes=[self.cD])
            self.cH = Buf(self.sb(es, "cH", [128, 128], BF16))
            trk.op("vector", lambda h: h.memset(self.cH.ap[:, :], 1.0 / DH), writes=[self.cH])

            tok8 = [(i * 512, 512) for i in range(8)]
            c512 = lambda n: [[(i * 512, 512)] for i in range(n)]
            self.bankgen(rel, oh, ps)
            with contextlib.ExitStack() as e1:
                hT = self.sb(e1, "hT", [128, 16, S + 2], BF16)
                hparts = [Buf(None) for _ in range(16)]
                self.phase_norm(xT, "attn_norm0", hT, hparts, 0, ps[6:8])
                with contextlib.ExitStack() as e2:
                    gq = Buf(self.sb(e2, "gq", [128, 4], F32))
                    trk.dma("sync", out=gq.ap[:, :], in_=gq0, writes=[gq])
                    hr = self.make_headrms_epi(e2, "p0", ps[4:6])
                    cp = self.make_copy_epi(e2, "p0c")
                    sg = self.make_copy_epi(e2, "p0g", dt=F32, func=AF.Sigmoid, nbuf=2)

                    def epi(es_, c, m, tk, t0, n, p):
                        if c < 2048:
                            hr(p, n, gq.ap[:, 0:1], gq, qT0[c // 128, :, t0:t0 + n])
                        elif c < 2048 + 1024:
                            cp(p, m, n, kvraw[(c - 2048) // 128, :, t0:t0 + n])
                        elif c < 2048 + 1536:
                            hr(p, n, gq.ap[:, 2:3], gq, kT_slc[(c - 3072) // 128, :, t0:t0 + n])
                        elif 4096 <= c < 4608:
                            hr(p, n, gq.ap[:, 3:4], gq, kT_win[(c - 4096) // 128, :, t0:t0 + n])
                        else:
                            sg(p, m, n, gatesT[0:m, t0:t0 + n])
                    cols = [[(i * 512, 512)] for i in (0, 1, 2, 3, 4, 5, 6, 8)] + [[(5120, 48)]]
                    self.gemm_fm(hT, hparts, 0, 16, w_in, cols, tok8, epi, ps[0:4], 512, "win")
                self.gemm_tm(hT, hparts, 0, w_in, 3584, v_slc, ps[0:4], "vs")
                self.gemm_tm(hT, hparts, 0, w_in, 4608, v_win, ps[0:4], "vw")
            if upto >= 2:
                with contextlib.ExitStack() as e2:
                    gq = Buf(self.sb(e2, "gqc", [128, 4], F32))
                    trk.dma("sync", out=gq.ap[:, :], in_=gq0, writes=[gq])
                    self.phase_compress(kvraw, cmp_w1, cmp_w2, cmp_pos, gq, kcmpT, vcmp, ps)
            while self.bgwork:
                self.bgwork.pop(0)()
            trk.barrier(allbg=True)
            if upto >= 3:
                self.phase_attn0(qT0, kT_slc, kT_win, v_slc, v_win, kcmpT, vcmp, gatesT, oT0, selc, expand, ovl, cst, ps)
            if upto >= 4:
                with contextlib.ExitStack() as e1:
                    hT = self.sb(e1, "hT", [128, 16, S + 2], BF16)
                    hparts = [Buf(None) for _ in range(16)]
                    self.load_hT(oT0, hT, hparts)
                    rsd = self.make_resid_epi(e1, "wo0", xT, x1T)
                    self.gemm_fm(hT, hparts, 0, 16, a_w_out, c512(4), tok8, rsd, ps[0:4], 512, "wo0")
            if upto >= 5:
                self.ffn(0, x1T, x2T, "ffn_norm0", w_up[0], cw[0], w_dn[0], actT, ps)
            if upto >= 6:
                with contextlib.ExitStack() as e1:
                    hT = self.sb(e1, "hT", [128, 16, S + 2], BF16)
                    hparts = [Buf(None) for _ in range(16)]
                    gq = Buf(self.sb(e1, "gq1", [128, 4], F32))
                    trk.dma("sync", out=gq.ap[:, :], in_=gq1, writes=[gq])
                    self.phase_norm(x2T, "kv_norm", hT, hparts, 0, ps[6:8])
                    with contextlib.ExitStack() as e2:
                        hr = self.make_headrms_epi(e2, "p1k", ps[4:6])

                        def epik(es_, c, m, tk, t0, n, p):
                            hr(p, n, gq.ap[:, 0:1], gq, kT_sh[c // 128, :, t0:t0 + n])
                        self.gemm_fm(hT, hparts, 0, 16, kv_w, c512(1), tok8, epik, ps[0:4], 512, "kvk")
                    self.gemm_tm(hT, hparts, 0, kv_w, 512, v_sh, ps[0:4], "vsh")
                    self.phase_norm(x2T, "attn_norm1", hT, hparts, 0, ps[6:8])
                    with contextlib.ExitStack() as e2:
                        hr = self.make_headrms_epi(e2, "p1q", ps[4:6])

                        def epiq(es_, c, m, tk, t0, n, p):
                            gi = c // 2048
                            hr(p, n, gq.ap[:, 1 + gi:2 + gi], gq, qT1[c // 128, :, t0:t0 + n])
                        self.gemm_fm(hT, hparts, 0, 16, b_w_q, c512(12), tok8, epiq, ps[0:4], 512, "bwq")
            if upto >= 7:
                self.phase_attn1(qT1, kT_sh, v_sh, oT1, ps)
            if upto >= 8:
                with contextlib.ExitStack() as e1:
                    hT = self.sb(e1, "hT", [128, 16, S + 2], BF16)
                    hparts = [Buf(None) for _ in range(16)]
                    self.load_hT(oT1, hT, hparts)
                    rsd = self.make_resid_epi(e1, "wo1", x2T, x3T)
                    self.gemm_fm(hT, hparts, 0, 16, b_w_out, c512(4), tok8, rsd, ps[0:4], 512, "wo1")
                self.ffn(1, x3T, outT, "ffn_norm1", w_up[1], cw[1], w_dn[1], actT, ps)
            trk.barrier()
        return nc


NEGV = 3840.0


def t5_bucket_np(n):
    n = np.maximum(np.asarray(n), 0).astype(np.int32)
    lr = np.log(np.maximum(n, 1).astype(np.float32) / np.float32(16)) / np.float32(math.log(4096 / 16))
    large = 16 + (lr * np.float32(16)).astype(np.int32)
    return np.where(n < 16, n, np.minimum(large, 31))


def host_consts():
    f = np.float32
    c = {}
    e = np.arange(GLEN)
    d = e - GOFF
    oh = np.zeros((5, 32, GLEN), f)
    specs = [(d >= 0, d), ((d >= 0) & (d <= 511), d), ((d >= 0) & (d <= 128), d),
             ((d >= 0) & (d <= 128), d * 4), ((d >= 0) & (d <= 128), d * 16)]
    for t, (valid, dist) in enumerate(specs):
        bk = t5_bucket_np(dist)
        oh[t, bk[valid], e[valid]] = 1.0
    c["onehot"] = oh
    cst = np.zeros((128, 1024), f)
    cst[:, 0:128] = np.eye(128, dtype=f)
    c["consts"] = cst
    p = np.arange(128)[:, None, None]
    i = np.arange(32)[None, :, None]
    blk = np.arange(64)[None, None, :]
    t = 128 * i + p
    cur = t // 64
    forced = (blk == 0) | (blk == cur) | (blk == cur - 1)
    causal = blk * 64 <= t
    sc = np.where(causal, np.where(forced, 1e4, 0.0), -1e30).astype(f)
    c["selc"] = np.ascontiguousarray(sc.reshape(128, 32 * 64))
    ex = np.zeros((128, 32, 128), f)
    for kt in range(32):
        ex[2 * kt, kt, 0:64] = NEGV
        ex[2 * kt + 1, kt, 64:128] = NEGV
    c["expand"] = ex.reshape(128, 32 * 128)
    n = np.arange(256)[:, None]
    j = np.arange(64)[None, :]
    ovm = np.maximum(np.minimum(16 * n + 32, 64 * j + 64) - np.maximum(16 * n, 64 * j), 0).astype(f) / 32.0
    ovm = np.concatenate([ovm, np.ones((256, 1), f)], axis=1)
    ovm[255] = 0.0
    c["ovl"] = np.ascontiguousarray(ovm.reshape(2, 128, 65).transpose(1, 0, 2).reshape(128, 130))
    return c


def host_all(inp, b):
    f = np.float32
    d = {}
    d["xT"] = np.ascontiguousarray(inp["x"][b].T)
    d["rel_bias"] = np.ascontiguousarray(inp["rel_bias"], f)

    def g16(v):
        return np.ascontiguousarray(np.asarray(v, f).reshape(16, 128).T)
    for i in range(2):
        d["attn_norm%d" % i] = g16(inp["attn_norm"][i])
        d["ffn_norm%d" % i] = g16(inp["ffn_norm"][i])
        d["ffn_w_up%d" % i] = inp["ffn_w_up"][i]
        d["ffn_w_down%d" % i] = inp["ffn_w_down"][i]
        d["ffn_conv%d" % i] = np.ascontiguousarray(inp["ffn_conv"][i].reshape(3, 88, 128).transpose(2, 1, 0))
    d["kv_norm"] = g16(inp["kv_norm"])
    d["a_w_in"] = inp["a_w_in"][0]
    kn = inp["a_k_norm"][0]
    d["gq0"] = np.ascontiguousarray(np.stack([inp["a_q_norm"][0], kn[0], kn[1], kn[2]], axis=1), f)
    d["cmp_posT"] = np.ascontiguousarray(inp["a_cmp_pos"][0].transpose(0, 2, 1))
    d["a_cmp_w1"] = inp["a_cmp_w1"][0]
    d["a_cmp_w2"] = inp["a_cmp_w2"][0]
    d["a_w_out"] = inp["a_w_out"][0]
    d["kv_w"] = inp["kv_w"]
    bq = inp["b_q_norm"][0]
    d["gq1"] = np.ascontiguousarray(np.stack([inp["kv_k_norm"], bq[0], bq[1], bq[2]], axis=1), f)
    d["b_w_q"] = inp["b_w_q"][0]
    d["b_w_out"] = inp["b_w_out"][0]
    d.update(host_consts())
    return d


_CACHE = {}


def kernel(**inp):
    inp = {k: np.asarray(v) for k, v in inp.items()}
    if "nc" not in _CACHE:
        kk = K()
        _CACHE["nc"] = kk.build()
    nc = _CACHE["nc"]
    maps = [host_all(inp, b) for b in range(8)]
    res = run_bass_kernel_spmd(nc, maps, core_ids=list(range(8)))
    out = np.empty((8, S, D), np.float32)
    for b in range(8):
        out[b] = np.asarray(res.results[b]["outT"]).T
    return out
```

```python
import contextlib
import math
import numpy as np
import concourse.bass as bass
import concourse.mybir as mybir
from concourse.bass_utils import run_bass_kernel_spmd

F32 = mybir.dt.float32
BF16 = mybir.dt.bfloat16
AF = mybir.ActivationFunctionType
ALU = mybir.AluOpType

S = 4096
D = 2048
H = 16
DH = 128
G = 4
DFF = 5632
NCMP = 255
EPS = 1e-6
SCALE = DH ** -0.5
A_IN = 5168
GLEN = 6784
GOFF = 2560
NEG = 240.0


class Buf:
    def __init__(self, ap):
        self.ap = ap
        self.w = {}
        self.r = {}
        self.ds = None
        self.bg = False


class Trk:
    def __init__(self, nc, es):
        self.nc = nc
        self.es = es
        self.engs = {}
        for n in ("tensor", "vector", "scalar", "gpsimd", "sync"):
            sem = es.enter_context(nc.semaphore("e_" + n))
            self.engs[n] = dict(h=getattr(nc, n), sem=sem, cnt=0, waited={}, pend=False)
        self.dpool = []
        self.dlive = []
        self.nd = 0

    def _wait(self, eng, deps):
        e = self.engs[eng]
        for k, (sem, val) in deps.items():
            if eng == "tensor" and k == "tensor":
                continue
            if e["waited"].get(k, 0) < val:
                e["h"].wait_ge(sem, val)
                e["waited"][k] = val

    @staticmethod
    def _deps(reads, writes):
        d = {}
        for b in reads:
            for k, sv in b.w.items():
                if k not in d or d[k][1] < sv[1]:
                    d[k] = sv
        for b in writes:
            for src in (b.w, b.r):
                for k, sv in src.items():
                    if k not in d or d[k][1] < sv[1]:
                        d[k] = sv
        return d

    def op(self, eng, fn, reads=(), writes=(), inc=True, skip_self=False):
        e = self.engs[eng]
        d = self._deps(reads, writes)
        if skip_self:
            d.pop(eng, None)
        self._wait(eng, d)
        ins = fn(e["h"])
        if inc:
            e["cnt"] += 1
            ins.then_inc(e["sem"], 1)
            val = e["cnt"]
        else:
            val = e["cnt"] + 1
        for b in writes:
            b.w = {eng: (e["sem"], val)}
            b.r = {}
        for b in reads:
            b.r[eng] = (e["sem"], val)
        return ins

    def _dsem(self, b):
        if b.ds is None:
            if self.dpool:
                b.ds = self.dpool.pop()
            else:
                sem = self.es.enter_context(self.nc.semaphore("d%d" % self.nd))
                b.ds = [sem, 0, "d%d" % self.nd]
                self.nd += 1
            self.dlive.append(b)
        return b.ds

    def dma(self, eng, out, in_, reads=(), writes=()):
        e = self.engs[eng]
        self._wait(eng, self._deps(reads, writes))
        ins = e["h"].dma_start(out=out, in_=in_)
        bs = list(reads) + list(writes)
        assert len(bs) == 1
        b = bs[0]
        ds = self._dsem(b)
        ds[1] += 16
        ins.then_inc(ds[0], 16)
        if writes:
            b.w = {ds[2]: (ds[0], ds[1])}
            b.r = {}
        else:
            b.r[ds[2]] = (ds[0], ds[1])
        return ins

    def barrier(self, allbg=False):
        sp = self.engs["sync"]
        keep = []
        for b in self.dlive:
            if b.bg and not allbg:
                keep.append(b)
                continue
            ds = b.ds
            if sp["waited"].get(ds[2], 0) < ds[1]:
                sp["h"].wait_ge(ds[0], ds[1])
                sp["waited"][ds[2]] = ds[1]
            self.dpool.append(ds)
            b.ds = None
        self.dlive = keep
        names = list(self.engs)
        for n in names:
            e = self.engs[n]
            for m in names:
                if m == n or m == "sync":
                    continue
                o = self.engs[m]
                if e["waited"].get(m, 0) < o["cnt"]:
                    e["h"].wait_ge(o["sem"], o["cnt"])
                    e["waited"][m] = o["cnt"]
        sp["cnt"] += 1
        sp["h"].sem_inc(sp["sem"], 1)
        for n in names:
            if n == "sync":
                continue
            e = self.engs[n]
            e["h"].wait_ge(sp["sem"], sp["cnt"])
            e["waited"]["sync"] = sp["cnt"]


def ap3(ap, dims):
    return bass.AP(ap.tensor, ap.offset, [list(ap.ap[0])] + [list(d) for d in dims])


class K:
    def __init__(self, dbg=None):
        self.dbg = dbg or {}
        self.nc = bass.Bass("TRN2", target_bir_lowering=False)
        self.dr = {}

    def din(self, name, shape, dt=F32):
        t = self.nc.dram_tensor(name, list(shape), dt, kind="ExternalInput").ap()
        self.dr[name] = t
        return t

    def dscr(self, name, shape, dt):
        kind = "ExternalOutput" if name in self.dbg else "Internal"
        t = self.nc.dram_tensor(name, list(shape), dt, kind=kind).ap()
        self.dr[name] = t
        return t

    def sb(self, es, name, shape, dt):
        self.uid = getattr(self, "uid", 0) + 1
        return es.enter_context(self.nc.sbuf_tensor("%s_u%d" % (name, self.uid), list(shape), dt))

    def phase_norm(self, xT, gname, hT, hparts, pad, psb):
        nc, trk = self.nc, self.trk
        TT = 512
        with contextlib.ExitStack() as es:
            xt = [Buf(self.sb(es, "nxt%d" % i, [128, 16, TT], F32)) for i in range(2)]
            sq = [Buf(self.sb(es, "nsq%d" % i, [128, TT], BF16)) for i in range(3)]
            rs = [Buf(self.sb(es, "nrs%d" % i, [128, TT], F32)) for i in range(2)]
            g = Buf(self.sb(es, "ng", [128, 16], F32))
            trk.dma("sync", out=g.ap[:, :], in_=self.dr[gname], writes=[g])
            xv = xT.rearrange("(kc p) t -> p kc t", p=128)
            nt = S // TT
            trk.dma("sync", out=xt[0].ap[:, :, :], in_=xv[:, :, 0:TT], writes=[xt[0]])
            for i in range(nt):
                x_, r_ = xt[i % 2], rs[i % 2]
                if i + 1 < nt:
                    trk.dma("sync" if i % 2 == 1 else "gpsimd", out=xt[(i + 1) % 2].ap[:, :, :],
                            in_=xv[:, :, (i + 1) * TT:(i + 2) * TT], writes=[xt[(i + 1) % 2]])
                ps = psb[i % 2]
                for kc in range(16):
                    s_ = sq[kc % 3]
                    trk.op("scalar", lambda h: h.activation(out=s_.ap[:, :], in_=x_.ap[:, kc, :], func=AF.Square),
                           reads=[x_], writes=[s_])
                    trk.op("tensor", lambda h: h.matmul(ps.ap[:, 0:TT], lhsT=self.cD.ap[:, :], rhs=s_.ap[:, :],
                                                        start=(kc == 0), stop=(kc == 15)),
                           reads=[s_, self.cD], writes=[ps])
                trk.op("scalar", lambda h: h.activation(out=r_.ap[:, :], in_=ps.ap[:, 0:TT], func=AF.Ln,
                                                        bias=self.epsc.ap[:, 0:1], scale=1.0),
                       reads=[ps, self.epsc], writes=[r_])
                trk.op("scalar", lambda h: h.activation(out=r_.ap[:, :], in_=r_.ap[:, :], func=AF.Exp, scale=-0.5),
                       reads=[r_], writes=[r_])
                hw = [hparts[2 * i], hparts[2 * i + 1]]
                for kc in range(16):
                    trk.op("vector", lambda h: h.scalar_tensor_tensor(out=hT[:, kc, pad + i * TT: pad + (i + 1) * TT],
                                                                      in0=x_.ap[:, kc, :], scalar=g.ap[:, kc:kc + 1],
                                                                      in1=r_.ap[:, :], op0=ALU.mult, op1=ALU.mult),
                           reads=[x_, r_, g], writes=hw, skip_self=(kc > 0))
            trk.barrier()

    def hp(self, hparts, pad, t0, n):
        a = max(t0 - pad, 0) // 256
        b = (t0 + n - 1 - pad) // 256
        return [hparts[i] for i in range(a, b + 1)]

    def gemm_fm(self, hT, hparts, pad, KC, W, col_tiles, tok_tiles, epilogue, psA, wt_cols, wname, after=None, barrier=True):
        nc, trk = self.nc, self.trk
        with contextlib.ExitStack() as es:
            wb = [Buf(self.sb(es, "%s_w%d" % (wname, i), [128, KC, wt_cols], BF16)) for i in range(2)]
            Wv = W.rearrange("(kc p) n -> p kc n", p=128)

            def load(ti):
                off = 0
                kstep = 11 if KC == 44 else 16
                for (c0, wd) in col_tiles[ti]:
                    for k0 in range(0, KC, kstep):
                        trk.dma("gpsimd", out=wb[ti % 2].ap[:, k0:k0 + kstep, off:off + wd],
                                in_=Wv[:, k0:k0 + kstep, c0:c0 + wd], writes=[wb[ti % 2]])
                    off += wd

            load(0)
            k = 0
            pend = []
            for ti, pieces in enumerate(col_tiles):
                if ti + 1 < len(col_tiles):
                    load(ti + 1)
                wbuf = wb[ti % 2]
                off = 0
                for (c0, wd) in pieces:
                    for nb in range(0, wd, 128):
                        m = min(128, wd - nb)
                        for tk, (t0, n) in enumerate(tok_tiles):
                            ps = psA[k % len(psA)]
                            k += 1
                            rd = [wbuf] + self.hp(hparts, pad, t0, n)
                            for kc in range(KC):
                                trk.op("tensor", lambda h: h.matmul(ps.ap[0:m, 0:n],
                                                                    lhsT=wbuf.ap[:, kc, off + nb:off + nb + m],
                                                                    rhs=hT[:, kc, t0:t0 + n], start=(kc == 0),
                                                                    stop=(kc == KC - 1)),
                                       reads=rd, writes=[ps], inc=(kc == KC - 1))
                            if pend:
                                pend.pop()()
                            pend.append(lambda a=(es, c0 + nb, m, tk, t0, n, ps): epilogue(*a))
                            if getattr(self, "bgwork", None) and k % 3 == 0:
                                self.bgwork.pop(0)()
                    off += wd
            if pend:
                pend.pop()()
            if after:
                after(es)
            if barrier:
                trk.barrier()

    def gemm_tm(self, hT, hparts, pad, W, c0, out_dram, psA, name):
        nc, trk = self.nc, self.trk
        with contextlib.ExitStack() as es:
            wb = Buf(self.sb(es, name + "_w", [128, 16, 512], BF16))
            Wv = W.rearrange("(kc p) n -> p kc n", p=128)
            trk.dma("gpsimd", out=wb.ap[:, :, :], in_=Wv[:, :, c0:c0 + 512], writes=[wb])
            ob = [Buf(self.sb(es, name + "_o%d" % i, [128, 512], BF16)) for i in range(3)]
            for tt in range(S // 128):
                ps = psA[tt % len(psA)]
                t0 = pad + tt * 128
                rd = [wb] + self.hp(hparts, pad, t0, 128)
                for kc in range(16):
                    trk.op("tensor", lambda h: h.matmul(ps.ap[:, :], lhsT=hT[:, kc, t0:t0 + 128], rhs=wb.ap[:, kc, :],
                                                        start=(kc == 0), stop=(kc == 15)),
                           reads=rd, writes=[ps], inc=(kc == 15))
                o = ob[tt % 3]
                trk.op("scalar", lambda h: h.activation(out=o.ap[:, :], in_=ps.ap[:, :], func=AF.Copy),
                       reads=[ps], writes=[o])
                trk.dma("sync", out=out_dram[tt * 128:(tt + 1) * 128, :], in_=o.ap[:, :], reads=[o])
            trk.barrier()

    def make_headrms_epi(self, es0, tag, psR):
        trk = self.trk
        sq = [Buf(self.sb(es0, tag + "_sq%d" % i, [128, 512], BF16)) for i in range(2)]
        rs = [Buf(self.sb(es0, tag + "_rs%d" % i, [128, 512], F32)) for i in range(2)]
        ob = [Buf(self.sb(es0, tag + "_ob%d" % i, [128, 512], BF16)) for i in range(3)]
        st = dict(i=0)

        def f(ps, n, gcol, gbuf, dst):
            i = st["i"]
            st["i"] += 1
            s_, r_, o_ = sq[i % 2], rs[i % 2], ob[i % 3]
            p2 = psR[i % len(psR)]
            trk.op("scalar", lambda h: h.activation(out=s_.ap[:, 0:n], in_=ps.ap[:, 0:n], func=AF.Square),
                   reads=[ps], writes=[s_])
            trk.op("tensor", lambda h: h.matmul(p2.ap[:, 0:n], lhsT=self.cH.ap[:, :], rhs=s_.ap[:, 0:n],
                                                start=True, stop=True), reads=[s_, self.cH], writes=[p2])
            trk.op("scalar", lambda h: h.activation(out=r_.ap[:, 0:n], in_=p2.ap[:, 0:n], func=AF.Ln,
                                                    bias=self.epsc.ap[:, 0:1], scale=1.0),
                   reads=[p2, self.epsc], writes=[r_])
            trk.op("scalar", lambda h: h.activation(out=r_.ap[:, 0:n], in_=r_.ap[:, 0:n], func=AF.Exp, scale=-0.5),
                   reads=[r_], writes=[r_])
            trk.op("vector", lambda h: h.scalar_tensor_tensor(out=o_.ap[:, 0:n], in0=ps.ap[:, 0:n], scalar=gcol,
                                                              in1=r_.ap[:, 0:n], op0=ALU.mult, op1=ALU.mult),
                   reads=[ps, r_, gbuf], writes=[o_])
            trk.dma("sync", out=dst, in_=o_.ap[:, 0:n], reads=[o_])
        return f

    def make_copy_epi(self, es0, tag, dt=BF16, func=AF.Copy, nbuf=3):
        trk = self.trk
        ob = [Buf(self.sb(es0, tag + "_cb%d" % i, [128, 512], dt)) for i in range(nbuf)]
        st = dict(i=0)

        def f(ps, m, n, dst):
            o_ = ob[st["i"] % nbuf]
            st["i"] += 1
            trk.op("scalar", lambda h: h.activation(out=o_.ap[0:m, 0:n], in_=ps.ap[0:m, 0:n], func=func),
                   reads=[ps], writes=[o_])
            trk.dma("sync", out=dst, in_=o_.ap[0:m, 0:n], reads=[o_])
        return f

    def make_resid_epi(self, es0, tag, xin, xout):
        trk = self.trk
        xb = [Buf(self.sb(es0, tag + "_xb%d" % i, [128, 512], F32)) for i in range(3)]
        st = dict(i=0)

        def f(es, c0, m, tk, t0, n, ps):
            b = xb[st["i"] % 3]
            st["i"] += 1
            trk.dma("sync", out=b.ap[:, 0:n], in_=xin[c0:c0 + 128, t0:t0 + n], writes=[b])
            trk.op("vector", lambda h: h.tensor_tensor(out=b.ap[:, 0:n], in0=ps.ap[:, 0:n], in1=b.ap[:, 0:n],
                                                       op=ALU.add), reads=[ps, b], writes=[b])
            trk.dma("sync", out=xout[c0:c0 + 128, t0:t0 + n], in_=b.ap[:, 0:n], reads=[b])
        return f


    SK = {"slc": (0, 1, 4480, 4736, 2049), "win": (1, 1, 1408, 1536, 2049), "cmp": (0, 16, 4096, 6144, 497),
          "d0": (2, 1, 1024, 1152, 2049), "d1": (3, 1, 1024, 1152, 2049), "d2": (4, 1, 1024, 1152, 2049)}

    def bankgen(self, rel, oh, ps):
        nc, trk = self.nc, self.trk
        self.Gd = self.dscr("Gd", [5, 16, GLEN], F32)
        self.skew = {k: self.dscr("sk_" + k, [16, 129 * v[3]], F32) for k, v in self.SK.items()}
        with contextlib.ExitStack() as es:
            rb = Buf(self.sb(es, "bg_rb", [32, 16], F32))
            eb = Buf(self.sb(es, "bg_eb", [32, 16], F32))
            trk.dma("sync", out=rb.ap[:, :], in_=rel, writes=[rb])
            trk.op("scalar", lambda h: h.activation(out=eb.ap[:, :], in_=rb.ap[:, :], func=AF.Exp), reads=[rb], writes=[eb])
            oht = [Buf(self.sb(es, "bg_oh%d" % i, [32, 512], F32)) for i in range(6)]
            gt = [Buf(self.sb(es, "bg_gt%d" % i, [16, 512], F32)) for i in range(6)]
            chunks = [(ty, c, min(512, GLEN - c)) for ty in range(5) for c in range(0, GLEN, 512)]

            def gload(k):
                ty, c, n = chunks[k]
                trk.dma("sync", out=oht[k % 6].ap[:, 0:n], in_=oh[ty, :, c:c + n], writes=[oht[k % 6]])
            for k in range(min(5, len(chunks))):
                gload(k)
            for k, (ty, c, n) in enumerate(chunks):
                o_, g_, p_ = oht[k % 6], gt[k % 6], ps[k % 6]
                trk.op("tensor", lambda h: h.matmul(p_.ap[0:16, 0:n], lhsT=eb.ap[:, :], rhs=o_.ap[:, 0:n],
                                                    start=True, stop=True), reads=[eb, o_], writes=[p_])
                trk.op("vector", lambda h: h.tensor_copy(out=g_.ap[:, 0:n], in_=p_.ap[0:16, 0:n]), reads=[p_], writes=[g_])
                trk.dma("sync", out=self.Gd[ty, :, c:c + n], in_=g_.ap[:, 0:n], reads=[g_])
                if k + 5 < len(chunks):
                    gload(k + 5)
            trk.barrier()
            dummy = Buf(None)
            dummy.bg = True
            self.bgwork = []
            for name, (ty, s_, Lr, P, g0) in self.SK.items():
                Lw = Lr + 127 * s_
                for h in range(16):
                    dst = bass.AP(self.skew[name].tensor, h * 129 * P, [[P + s_, 128], [1, Lw]])
                    src = bass.AP(self.Gd.tensor, (ty * 16 + h) * GLEN + g0, [[0, 128], [1, Lw]])
                    self.bgwork.append(lambda dst=dst, src=src: trk.dma("sync", out=dst, in_=src, writes=[dummy]))

    def bank_load(self, name, h, buf):
        ty, s_, Lr, P, g0 = self.SK[name]
        src = bass.AP(self.skew[name].tensor, h * 129 * P + 127 * s_, [[P, 128], [1, Lr]])
        self.trk.dma("sync", out=buf.ap[:, 0:Lr], in_=src, writes=[buf])

    def attn_core(self, W, QT, qap, nq, kv, num, den, consume=None):
        trk = self.trk
        sS, pf, pb = W["sS"], W["pf"], W["pb"]
        n = len(kv)
        st = W["st"]

        def emitS(i):
            t = kv[i]
            s_ = sS[(st["s"] + i) % len(sS)]
            mk = t.get("mask")
            trk.op("tensor", lambda h: h.matmul(s_.ap[:, 0:nq], lhsT=t["KT"], rhs=qap, start=True, stop=(mk is None)),
                   reads=[t["kb"], QT], writes=[s_], inc=(mk is None))
            if mk is not None:
                trk.op("tensor", lambda h: h.matmul(s_.ap[:, 0:nq], lhsT=mk[0], rhs=mk[1], start=False, stop=True),
                       reads=mk[2], writes=[s_])
        LA = len(sS) - 1
        for i in range(min(LA, n)):
            emitS(i)
        for i in range(n):
            if i + LA < n:
                emitS(i + LA)
            t = kv[i]
            s_ = sS[(st["s"] + i) % len(sS)]
            f_ = pf[(st["p"] + i) % len(pf)]
            b_ = pb[(st["p"] + i) % len(pb)]
            trk.op("scalar", lambda h: h.activation(out=f_.ap[:, 0:nq], in_=s_.ap[:, 0:nq], func=AF.Exp, scale=SCALE),
                   reads=[s_], writes=[f_])
            trk.op("vector", lambda h: h.tensor_tensor(out=b_.ap[:, 0:nq], in0=f_.ap[:, 0:nq], in1=t["E"], op=ALU.mult),
                   reads=[f_, t["eb"]], writes=[b_])
            if consume is not None:
                consume(i, b_)
            else:
                trk.op("tensor", lambda h: h.matmul(num.ap[:, 0:nq], lhsT=t["V"], rhs=b_.ap[:, 0:nq], start=(i == 0),
                                                    stop=(i == n - 1)), reads=[t["vb"], b_], writes=[num], inc=False)
                trk.op("tensor", lambda h: h.matmul(den.ap[:, 0:nq], lhsT=self.ones.ap[:, :], rhs=b_.ap[:, 0:nq],
                                                    start=(i == 0), stop=(i == n - 1)), reads=[self.ones, b_],
                       writes=[den, num])
        st["s"] += n
        st["p"] += n

    def run_pipe(self, W, tasks, defer=3):
        trk = self.trk
        sS, pf, pb = W["sS"], W["pf"], W["pb"]
        n = len(tasks)
        LA = len(sS) - 1

        def emitS(i):
            t = tasks[i]
            s_ = sS[i % len(sS)]
            mk = t.get("mask")
            nq = t["nq"]
            trk.op("tensor", lambda h: h.matmul(s_.ap[:, 0:nq], lhsT=t["KT"], rhs=t["qap"], start=True, stop=(mk is None)),
                   reads=[t["kb"], t["QT"]], writes=[s_], inc=(mk is None))
            if mk is not None:
                trk.op("tensor", lambda h: h.matmul(s_.ap[:, 0:nq], lhsT=mk[0], rhs=mk[1], start=False, stop=True),
                       reads=mk[2], writes=[s_])
        posts = []
        for i in range(min(LA, n)):
            emitS(i)
        for i in range(n):
            t = tasks[i]
            if t.get("hook"):
                t["hook"]()
            while posts and posts[0][0] <= i:
                posts.pop(0)[1]()
            if t["first"]:
                idx = [k for k, p in enumerate(posts) if any(b is t["num"] or b is t["den"] for b in p[2])]
                if idx:
                    for _ in range(idx[-1] + 1):
                        posts.pop(0)[1]()
            if i + LA < n:
                emitS(i + LA)
            nq = t["nq"]
            s_ = sS[i % len(sS)]
            f_ = pf[i % len(pf)]
            b_ = pb[i % len(pb)]
            num, den = t["num"], t["den"]
            numap = t.get("numap", None)
            denap = t.get("denap", None)
            if numap is None:
                numap = num.ap[:, 0:nq]
            if denap is None:
                denap = den.ap[:, 0:nq]
            trk.op("scalar", lambda h: h.activation(out=f_.ap[:, 0:nq], in_=s_.ap[:, 0:nq], func=AF.Exp, scale=SCALE),
                   reads=[s_], writes=[f_])
            trk.op("vector", lambda h: h.tensor_tensor(out=b_.ap[:, 0:nq], in0=f_.ap[:, 0:nq], in1=t["E"], op=ALU.mult),
                   reads=[f_, t["eb"]], writes=[b_])
            trk.op("tensor", lambda h: h.matmul(numap, lhsT=t["V"], rhs=b_.ap[:, 0:nq], start=t["first"],
                                                stop=t["last"]), reads=[t["vb"], b_], writes=[num], inc=False)
            dstart = t["first"] and (num is not den)
            trk.op("tensor", lambda h: h.matmul(denap, lhsT=self.ones.ap[:, :], rhs=b_.ap[:, 0:nq],
                                                start=dstart, stop=t["last"]), reads=[self.ones, b_],
                   writes=[den, num])
            if t["last"] and t.get("post"):
                pp = t["post"]
                if callable(pp):
                    posts.append((i + defer, pp, [num, den]))
                else:
                    for dly, fn, which in pp:
                        posts.append((i + dly, fn, [den] if which == "den" else ([num] if which == "num" else [num, den])))
                    posts.sort(key=lambda x: x[0])
        for p in posts:
            p[1]()

    def fin_gate(self, F, gate_src, nq):
        g_ = F["g"][F["gi"] % len(F["g"])]
        F["gi"] += 1
        self.trk.dma("sync", out=g_.ap[:, 0:nq], in_=gate_src, writes=[g_])
        return g_

    def fin_a(self, F, den, nq, g_):
        trk = self.trk
        r_ = F["r"][F["i"] % len(F["r"])]
        F["i"] += 1
        trk.op("scalar", lambda h: h.activation(out=r_.ap[:, 0:nq], in_=den.ap[:, 0:nq], func=AF.Ln,
                                                bias=self.tiny.ap[:, 0:1], scale=1.0), reads=[den, self.tiny], writes=[r_])
        trk.op("scalar", lambda h: h.activation(out=r_.ap[:, 0:nq], in_=r_.ap[:, 0:nq], func=AF.Exp, scale=-1.0),
               reads=[r_], writes=[r_])
        return r_

    def fin_b(self, F, acc, num, nq, r_, first, g_, out=None):
        trk = self.trk
        trk.op("vector", lambda h: h.tensor_tensor(out=r_.ap[:, 0:nq], in0=r_.ap[:, 0:nq], in1=g_.ap[:, 0:nq], op=ALU.mult),
               reads=[r_, g_], writes=[r_])
        if first:
            trk.op("vector", lambda h: h.tensor_tensor(out=acc.ap[:, 0:nq], in0=num.ap[:, 0:nq], in1=r_.ap[:, 0:nq],
                                                       op=ALU.mult), reads=[num, r_], writes=[acc])
        else:
            t_ = F["t"][F["ti"] % len(F["t"])]
            F["ti"] += 1
            trk.op("vector", lambda h: h.tensor_tensor(out=t_.ap[:, 0:nq], in0=num.ap[:, 0:nq], in1=r_.ap[:, 0:nq],
                                                       op=ALU.mult), reads=[num, r_], writes=[t_])
            dst = acc if out is None else out
            trk.op("vector", lambda h: h.tensor_tensor(out=dst.ap[:, 0:nq], in0=acc.ap[:, 0:nq], in1=t_.ap[:, 0:nq],
                                                       op=ALU.add), reads=[acc, t_], writes=[dst])

    def attn_work(self, es, tag):
        W = dict(st=dict(s=0, p=0))
        W["pf"] = [Buf(self.sb(es, tag + "_pf%d" % i, [128, 512], F32)) for i in range(4)]
        W["pb"] = [Buf(self.sb(es, tag + "_pb%d" % i, [128, 512], BF16)) for i in range(4)]
        return W

    def phase_compress(self, kvraw, cmp_w1, cmp_w2, cmp_pos, gq, kcmpT, vcmp, ps):
        trk = self.trk
        with contextlib.ExitStack() as es:
            w1 = Buf(self.sb(es, "c_w1", [128, 32, 128], BF16))
            w2 = Buf(self.sb(es, "c_w2", [128, 128], BF16))
            pos = Buf(self.sb(es, "c_pos", [128, 32], BF16))
            bias = Buf(self.sb(es, "c_bias", [128, 1], F32))
            raw = [Buf(self.sb(es, "c_raw%d" % i, [128, S], BF16)) for i in range(2)]
            gl = Buf(self.sb(es, "c_gl", [128, 256], BF16))
            sq = Buf(self.sb(es, "c_sq", [128, 256], BF16))
            rs = Buf(self.sb(es, "c_rs", [128, 256], F32))
            ko = Buf(self.sb(es, "c_ko", [128, 256], BF16))
            vo = Buf(self.sb(es, "c_vo", [128, 2, 128], BF16))
            trk.op("vector", lambda h: h.memset(gl.ap[:, :], 0.0), writes=[gl])
            trk.op("vector", lambda h: h.memset(ko.ap[:, :], 0.0), writes=[ko])
            k = 0
            for kv in range(2):
                trk.dma("gpsimd", out=w1.ap[:, :, :], in_=cmp_w1[kv].rearrange("(l p) n -> p l n", p=128), writes=[w1])
                trk.dma("gpsimd", out=w2.ap[:, :], in_=cmp_w2[kv], writes=[w2])
                trk.dma("gpsimd", out=pos.ap[:, :], in_=cmp_pos[kv], writes=[pos])
                pb_ = ps[0]
                for l in range(32):
                    trk.op("tensor", lambda h: h.matmul(pb_.ap[:, 0:1], lhsT=w1.ap[:, l, :], rhs=pos.ap[:, l:l + 1],
                                                        start=(l == 0), stop=(l == 31)), reads=[w1, pos], writes=[pb_],
                           inc=(l == 31))
                trk.op("vector", lambda h: h.tensor_copy(out=bias.ap[:, :], in_=pb_.ap[:, 0:1]), reads=[pb_], writes=[bias])
                for g in range(4):
                    r_ = raw[k % 2]
                    k += 1
                    trk.dma("sync", out=r_.ap[:, :], in_=kvraw[kv * 4 + g], writes=[r_])
                    p1 = ps[1]
                    for l in range(32):
                        trk.op("tensor", lambda h: h.matmul(p1.ap[:, 0:255], lhsT=w1.ap[:, l, :],
                                                            rhs=ap3(r_.ap[:, l:l + 1], [[16, 255]]), start=(l == 0),
                                                            stop=(l == 31)), reads=[w1, r_], writes=[p1], inc=(l == 31))
                    trk.op("scalar", lambda h: h.activation(out=gl.ap[:, 0:255], in_=p1.ap[:, 0:255],
                                                            func=AF.Gelu_apprx_tanh, bias=bias.ap[:, 0:1], scale=1.0),
                           reads=[p1, bias], writes=[gl])
                    if kv == 0:
                        p2 = ps[2]
                        trk.op("tensor", lambda h: h.matmul(p2.ap[:, 0:255], lhsT=w2.ap[:, :], rhs=gl.ap[:, 0:255],
                                                            start=True, stop=True), reads=[w2, gl], writes=[p2])
                        trk.op("scalar", lambda h: h.activation(out=sq.ap[:, 0:255], in_=p2.ap[:, 0:255], func=AF.Square),
                               reads=[p2], writes=[sq])
                        p3 = ps[3]
                        trk.op("tensor", lambda h: h.matmul(p3.ap[:, 0:255], lhsT=self.cH.ap[:, :], rhs=sq.ap[:, 0:255],
                                                            start=True, stop=True), reads=[self.cH, sq], writes=[p3])
                        trk.op("scalar", lambda h: h.activation(out=rs.ap[:, 0:255], in_=p3.ap[:, 0:255], func=AF.Sqrt,
                                                                bias=self.epsc.ap[:, 0:1], scale=1.0),
                               reads=[p3, self.epsc], writes=[rs])
                        trk.op("vector", lambda h: h.reciprocal(out=rs.ap[:, 0:255], in_=rs.ap[:, 0:255]), reads=[rs], writes=[rs])
                        trk.op("vector", lambda h: h.scalar_tensor_tensor(out=ko.ap[:, 0:255], in0=p2.ap[:, 0:255],
                                                                          scalar=gq.ap[:, 1:2], in1=rs.ap[:, 0:255],
                                                                          op0=ALU.mult, op1=ALU.mult),
                               reads=[p2, rs, gq], writes=[ko])
                        trk.dma("sync", out=kcmpT[g], in_=ko.ap[:, :], reads=[ko])
                    else:
                        for nt in range(2):
                            p2 = ps[2 + nt]
                            trk.op("tensor", lambda h: h.matmul(p2.ap[:, 0:128], lhsT=gl.ap[:, nt * 128:(nt + 1) * 128],
                                                                rhs=w2.ap[:, :], start=True, stop=True),
                                   reads=[w2, gl], writes=[p2])
                            trk.op("vector", lambda h: h.tensor_copy(out=vo.ap[:, nt, :], in_=p2.ap[:, 0:128]),
                                   reads=[p2], writes=[vo])
                        trk.dma("sync", out=vcmp[g], in_=vo.ap[:, :, :], reads=[vo])
            trk.barrier()

    def finalize(self, F, num, den, nq, gate_src, first):
        trk = self.trk
        i = F["i"]
        F["i"] += 1
        r_ = F["r"][i % 2]
        trk.op("vector", lambda h: h.tensor_scalar(out=r_.ap[:, 0:nq], in0=den.ap[:, 0:nq], scalar1=1e-30, scalar2=None,
                                                   op0=ALU.max), reads=[den], writes=[r_])
        trk.op("scalar", lambda h: h.activation(out=r_.ap[:, 0:nq], in_=r_.ap[:, 0:nq], func=AF.Ln), reads=[r_], writes=[r_])
        trk.op("scalar", lambda h: h.activation(out=r_.ap[:, 0:nq], in_=r_.ap[:, 0:nq], func=AF.Exp, scale=-1.0),
               reads=[r_], writes=[r_])
        if gate_src is not None:
            g_ = F["g"][i % 2]
            trk.dma("sync", out=g_.ap[:, 0:nq], in_=gate_src, writes=[g_])
            trk.op("vector", lambda h: h.tensor_tensor(out=r_.ap[:, 0:nq], in0=r_.ap[:, 0:nq], in1=g_.ap[:, 0:nq],
                                                       op=ALU.mult), reads=[r_, g_], writes=[r_])
        acc = F["acc"]
        if first:
            trk.op("vector", lambda h: h.tensor_tensor(out=acc.ap[:, 0:nq], in0=num.ap[:, 0:nq], in1=r_.ap[:, 0:nq],
                                                       op=ALU.mult), reads=[num, r_], writes=[acc])
        else:
            t_ = F["t"]
            trk.op("vector", lambda h: h.tensor_tensor(out=t_.ap[:, 0:nq], in0=num.ap[:, 0:nq], in1=r_.ap[:, 0:nq],
                                                       op=ALU.mult), reads=[num, r_], writes=[t_])
            trk.op("gpsimd", lambda h: h.tensor_tensor(out=acc.ap[:, 0:nq], in0=acc.ap[:, 0:nq], in1=t_.ap[:, 0:nq],
                                                       op=ALU.add), reads=[acc, t_], writes=[acc])

    def phase_attn0(self, qT0, kT_slc, kT_win, v_slc, v_win, kcmpT, vcmp, gatesT, oT0, selc, expand, ovl, cst, ps):
        trk = self.trk
        with contextlib.ExitStack() as es:
            ksl = Buf(self.sb(es, "a_ksl", [128, S], BF16))
            kwi = Buf(self.sb(es, "a_kwi", [128, S], BF16))
            vsl = Buf(self.sb(es, "a_vsl", [128, 32, 128], BF16))
            vwi = Buf(self.sb(es, "a_vwi", [128, 32, 128], BF16))
            kcm = Buf(self.sb(es, "a_kcm", [128, 256], BF16))
            vcm = Buf(self.sb(es, "a_vcm", [128, 2, 128], BF16))
            QT = [Buf(self.sb(es, "a_qt%d" % i, [128, S], BF16)) for i in range(2)]
            bss = [Buf(self.sb(es, "a_bs%d" % i, [128, 4480], F32)) for i in range(2)]
            bws = [Buf(self.sb(es, "a_bw%d" % i, [128, 1408], F32)) for i in range(2)]
            bcs = [Buf(self.sb(es, "a_bc%d" % i, [128, 4096], F32)) for i in range(2)]
            bc = bcs[0]
            hn = 0
            negT = Buf(self.sb(es, "a_negT", [128, S], BF16))
            trk.op("vector", lambda h: h.memset(negT.ap[:, :], 0.0), writes=[negT])
            negm = Buf(self.sb(es, "a_negm", [128, 32, 64], F32))
            score = Buf(self.sb(es, "a_score", [128, 32, 64], F32))
            expd = Buf(self.sb(es, "a_expd", [128, 32, 128], BF16))
            ov = Buf(self.sb(es, "a_ov", [128, 2, 65], BF16))
            ident = Buf(self.sb(es, "a_id", [128, 128], F32))
            m8a = Buf(self.sb(es, "a_m8a", [128, 8], F32))
            m8b = Buf(self.sb(es, "a_m8b", [128, 8], F32))
            sc2 = Buf(self.sb(es, "a_sc2", [128, 64], F32))
            recs = [Buf(self.sb(es, "a_rec%d" % i, [128, 4], F32)) for i in range(4)]
            imptmp = [Buf(self.sb(es, "a_imp%d" % i, [128, 4, 64], F32)) for i in range(2)]
            pi = 0
            W = self.attn_work(es, "a")
            W["sS"] = [ps[0], ps[1], ps[2]]
            F = dict(i=0, gi=0, ti=0, r=[Buf(self.sb(es, "a_r%d" % i, [128, 512], F32)) for i in range(3)],
                     g=[Buf(self.sb(es, "a_g%d" % i, [128, 512], F32)) for i in range(6)],
                     t=[Buf(self.sb(es, "a_t%d" % i, [128, 512], F32)) for i in range(2)], acc=None)
            accs = [Buf(self.sb(es, "a_acc%d" % i, [128, 512], F32)) for i in range(2)]
            obs = [Buf(self.sb(es, "a_ob%d" % i, [128, 512], BF16)) for i in range(2)]
            trk.dma("gpsimd", out=expd.ap[:, :, :], in_=expand.rearrange("p (k c) -> p k c", c=128), writes=[expd])
            trk.dma("gpsimd", out=ov.ap[:, :, :], in_=ovl.rearrange("p (k c) -> p k c", c=65), writes=[ov])
            trk.dma("sync", out=ident.ap[:, :], in_=cst[:, 0:128], writes=[ident])
            psI, psT = ps[3], ps[3]
            nb = 0
            qn = 0
            for g in range(4):
                trk.dma("sync", out=ksl.ap[:, :], in_=kT_slc[g], writes=[ksl])
                trk.dma("sync", out=kwi.ap[:, :], in_=kT_win[g], writes=[kwi])
                trk.dma("sync", out=vsl.ap[:, :, :], in_=v_slc.rearrange("(k p) c -> p k c", p=128)[:, :, g * 128:(g + 1) * 128], writes=[vsl])
                trk.dma("sync", out=vwi.ap[:, :, :], in_=v_win.rearrange("(k p) c -> p k c", p=128)[:, :, g * 128:(g + 1) * 128], writes=[vwi])
                trk.dma("sync", out=kcm.ap[:, :], in_=kcmpT[g], writes=[kcm])
                trk.dma("sync", out=vcm.ap[:, :, :], in_=vcmp[g], writes=[vcm])
                trk.dma("sync", out=score.ap[:, :, :], in_=selc.rearrange("p (k c) -> p k c", c=64), writes=[score])

                def cmp_kv(qt, bc):
                    lst = []
                    for nt in range(2):
                        if nt == 1 and qt <= 3:
                            continue
                        j0 = 512 * qt - 2048 * nt
                        lst.append(dict(KT=kcm.ap[:, nt * 128:(nt + 1) * 128], kb=kcm, V=vcm.ap[:, nt, :], vb=vcm,
                                        E=bc.ap[:, j0:j0 + 512], eb=bc, nt=nt))
                    return lst
                def loadp1(h_, slot_):
                    trk.dma("sync", out=QT[slot_].ap[:, :], in_=qT0[h_], writes=[QT[slot_]])
                    self.bank_load("cmp", h_, bcs[slot_])
                loadp1(4 * g, hn % 2)
                for hl in range(4):
                    h = 4 * g + hl
                    slot = hn % 2
                    hn += 1
                    Q, bc_ = QT[slot], bcs[slot]
                    if hl < 3:
                        loadp1(h + 1, (slot + 1) % 2)
                    for qt in range(8):
                        kvl = cmp_kv(qt, bc_)
                        held = []
                        self.attn_core(W, Q, Q.ap[:, qt * 512:(qt + 1) * 512], 512, kvl, None, None,
                                       consume=lambda i, b_: held.append(b_))
                        psI = ps[3 + pi % 2]
                        rec = recs[pi % 4]
                        tmp = imptmp[pi % 2]
                        pi += 1
                        for sub in range(4):
                            for i, t in enumerate(kvl):
                                b_ = held[i]
                                lastmm = (i == len(kvl) - 1)
                                trk.op("tensor", lambda hh: hh.matmul(psI.ap[:, sub * 65:(sub + 1) * 65],
                                                                      lhsT=b_.ap[:, sub * 128:(sub + 1) * 128],
                                                                      rhs=ov.ap[:, t["nt"], :], start=(i == 0), stop=lastmm),
                                       reads=[b_, ov], writes=[psI], inc=(lastmm and sub == 3))
                        den4 = ap3(psI.ap[:, 64:65], [[65, 4]])
                        imp4 = ap3(psI.ap[:, 0:1], [[65, 4], [1, 64]])
                        recb = ap3(rec.ap[:, 0:1], [[1, 4], [0, 64]])
                        trk.op("vector", lambda hh: hh.tensor_scalar(out=rec.ap[:, 0:4], in0=den4, scalar1=1e-30, scalar2=None,
                                                                     op0=ALU.max), reads=[psI], writes=[rec])
                        trk.op("vector", lambda hh: hh.reciprocal(out=rec.ap[:, 0:4], in_=rec.ap[:, 0:4]), reads=[rec], writes=[rec])
                        trk.op("vector", lambda hh: hh.tensor_tensor(out=tmp.ap[:, :, :], in0=imp4, in1=recb, op=ALU.mult),
                               reads=[psI, rec], writes=[tmp])
                        trk.op("vector", lambda hh: hh.tensor_tensor(out=score.ap[:, qt * 4:(qt + 1) * 4, :],
                                                                     in0=score.ap[:, qt * 4:(qt + 1) * 4, :], in1=tmp.ap[:, :, :],
                                                                     op=ALU.add), reads=[score, tmp], writes=[score])
                def loadhead(h, slot):
                    trk.dma("sync", out=QT[slot].ap[:, :], in_=qT0[h], writes=[QT[slot]])
                    self.bank_load("cmp", h, bcs[slot])
                    self.bank_load("slc", h, bss[slot])
                    self.bank_load("win", h, bws[slot])
                loadhead(4 * g, hn % 2)
                for qi in range(32):
                    trk.op("vector", lambda hh: hh.max(out=m8a.ap[:, :], in_=score.ap[:, qi, :]), reads=[score], writes=[m8a])
                    trk.op("vector", lambda hh: hh.match_replace(out=sc2.ap[:, :], in_to_replace=m8a.ap[:, :],
                                                                 in_values=score.ap[:, qi, :], imm_value=-3e38),
                           reads=[score, m8a], writes=[sc2])
                    trk.op("vector", lambda hh: hh.max(out=m8b.ap[:, :], in_=sc2.ap[:, :]), reads=[sc2], writes=[m8b])
                    trk.op("vector", lambda hh: hh.tensor_scalar(out=negm.ap[:, qi, :], in0=score.ap[:, qi, :],
                                                                 scalar1=m8b.ap[:, 7:8], scalar2=1.0, op0=ALU.is_ge,
                                                                 op1=ALU.subtract), reads=[score, m8b], writes=[negm])
                    trk.op("tensor", lambda hh: hh.transpose(out=psT.ap[0:64, (qi % 4) * 128:(qi % 4 + 1) * 128],
                                                             in_=negm.ap[:, qi, :], identity=ident.ap[:, :]),
                           reads=[negm, ident], writes=[psT])
                    if qi % 4 == 3:
                        c0 = (qi // 4) * 512
                        trk.op("scalar", lambda hh: hh.activation(out=negT.ap[0:64, c0:c0 + 512], in_=psT.ap[0:64, :], func=AF.Copy),
                               reads=[psT], writes=[negT])
                tasks = []
                for hl in range(4):
                    h = 4 * g + hl
                    slot = hn % 2
                    hn += 1
                    Q, bc_, bs_, bw_ = QT[slot], bcs[slot], bss[slot], bws[slot]
                    firsttask = True
                    for qt in range(8):
                        qap = Q.ap[:, qt * 512:(qt + 1) * 512]
                        acc = accs[qt % 2]
                        branches = []
                        lst = []
                        for nt in range(2):
                            if nt == 1 and qt <= 3:
                                continue
                            j0 = 512 * qt - 2048 * nt
                            lst.append(dict(KT=kcm.ap[:, nt * 128:(nt + 1) * 128], kb=kcm, V=vcm.ap[:, nt, :], vb=vcm,
                                            E=bc_.ap[:, j0:j0 + 512], eb=bc_))
                        branches.append(lst)
                        lst = []
                        for kt in range(0, 4 * qt + 4):
                            j0 = 512 * qt - 128 * kt + 384
                            off = max(0, kt - 4 * qt) * 128
                            n_ = 512 - off
                            lst.append(dict(KT=ksl.ap[:, kt * 128:(kt + 1) * 128], kb=ksl, V=vsl.ap[:, kt, :], vb=vsl,
                                            E=bs_.ap[:, j0 + off:j0 + 512], eb=bs_, off=off, nq=n_,
                                            mask=(expd.ap[:, kt, :], negT.ap[:, qt * 512 + off:(qt + 1) * 512], [expd, negT])))
                        branches.append(lst)
                        lst = []
                        kts_w = [kt for kt in range(max(0, 4 * qt - 4), 4 * qt + 4)]
                        kts_w = [4 * qt] + [kt for kt in kts_w if kt != 4 * qt]
                        for kt in kts_w:
                            j0 = 512 * qt - 128 * kt + 384
                            d_ = kt - 4 * qt
                            off = max(0, d_) * 128
                            n_ = (512 - off) if d_ >= 0 else min(512, 128 * (5 + d_))
                            lst.append(dict(KT=kwi.ap[:, kt * 128:(kt + 1) * 128], kb=kwi, V=vwi.ap[:, kt, :], vb=vwi,
                                            E=bw_.ap[:, j0 + off:j0 + off + n_], eb=bw_, off=off, nq=n_))
                        branches.append(lst)
                        gst = {}

                        def ghook(h=h, qt=qt, gst=gst):
                            for j in range(3):
                                gsrc = bass.AP(gatesT.tensor, (h * 3 + j) * S + qt * 512, [[0, 128], [1, 512]])
                                gst[j] = self.fin_gate(F, gsrc, 512)
                        qfirst = True
                        for j, kvl in enumerate(branches):
                            num, den = ps[3 + nb % 3], ps[6 + nb % 2]
                            nb += 1
                            st = {}

                            def postA(h=h, j=j, qt=qt, den=den, st=st, gst=gst):
                                st["r"] = self.fin_a(F, den, 512, gst[j])

                            def postB(h=h, j=j, qt=qt, num=num, acc=acc, st=st, gst=gst):
                                o_ = obs[qt % 2]
                                self.fin_b(F, acc, num, 512, st["r"], j == 0, gst[j], out=(o_ if j == 2 else None))
                                if j == 2:
                                    trk.dma("sync", out=oT0[h, :, qt * 512:(qt + 1) * 512], in_=o_.ap[:, :], reads=[o_])
                            post = [(4, postA, "den"), (6, postB, "num")]
                            for ti, t in enumerate(kvl):
                                off = t.get("off", 0)
                                n_ = t.get("nq", 512)
                                t.update(QT=Q, qap=Q.ap[:, qt * 512 + off:qt * 512 + off + n_], nq=n_, num=num, den=den,
                                         numap=num.ap[:, off:off + n_], denap=den.ap[:, off:off + n_],
                                         first=(ti == 0), last=(ti == len(kvl) - 1))
                                if ti == len(kvl) - 1:
                                    t["post"] = post
                                hooks = []
                                if qfirst:
                                    qfirst = False
                                    hooks.append(ghook)
                                if firsttask:
                                    firsttask = False
                                    if hl < 3:
                                        hooks.append(lambda hh=h + 1, sl=(slot + 1) % 2: loadhead(hh, sl))
                                if hooks:
                                    t["hook"] = (lambda hs=hooks: [f() for f in hs])
                                tasks.append(t)
                self.run_pipe(W, tasks)
            trk.barrier()

    def phase_attn1(self, qT1, kT_sh, v_sh, oT1, ps):
        trk = self.trk
        with contextlib.ExitStack() as es:
            kT = Buf(self.sb(es, "b_kT", [128, S], BF16))
            Vd = [Buf(self.sb(es, "b_v%d" % i, [128, 32, 128], BF16)) for i in range(3)]
            QT = [Buf(self.sb(es, "b_qt%d" % i, [128, S], BF16)) for i in range(2)]
            bk = [Buf(self.sb(es, "b_bk%d" % i, [128, 1024], F32)) for i in range(2)]
            naccs = [Buf(self.sb(es, "b_nacc%d" % i, [128, S], F32)) for i in range(2)]
            daccs = [Buf(self.sb(es, "b_dacc%d" % i, [128, S], F32)) for i in range(2)]
            obs = [Buf(self.sb(es, "b_ob%d" % i, [128, S], BF16)) for i in range(2)]
            W = self.attn_work(es, "b")
            W["sS"] = [ps[0], ps[1], ps[2]]
            dils = (1, 4, 16)
            nb = 0
            cn = 0

            def loadcombo(h, gi, slot):
                trk.dma("sync", out=QT[slot].ap[:, :], in_=qT1[gi * 16 + h], writes=[QT[slot]])
                self.bank_load("d%d" % gi, h, bk[slot])
            for g in range(4):
                trk.dma("sync", out=kT.ap[:, :], in_=kT_sh[g], writes=[kT])
                for gi, dil in enumerate(dils):
                    ntl = S // dil // 128
                    for res in range(dil):
                        src = bass.AP(v_sh.tensor, res * 512 + g * 128, [[dil * 512, 128], [dil * 128 * 512, ntl], [1, 128]])
                        trk.dma("sync", out=Vd[gi].ap[:, res * ntl:(res + 1) * ntl, :], in_=src, writes=[Vd[gi]])
                tasks = []
                combos = [(4 * g + hl, gi) for hl in range(4) for gi in range(3)]
                loadcombo(combos[0][0], combos[0][1], cn % 2)
                for ci, (h, gi) in enumerate(combos):
                    dil = dils[gi]
                    slot = cn % 2
                    cn += 1
                    Q, B = QT[slot], bk[slot]
                    nacc, dacc, ob = naccs[h % 2], daccs[h % 2], obs[h % 2]
                    L = S // dil
                    ntl = L // 128
                    nq = min(512, L)
                    firsttask = True
                    segs = [(res, a0) for res in range(dil) for a0 in range(0, L, nq)]
                    for si, (res, a0) in enumerate(segs):
                        qap = ap3(Q.ap[:, res + dil * a0:res + dil * a0 + 1], [[dil, nq]])
                        if nq == 256:
                            num = den = ps[3 + nb % 5]
                            numap, denap = num.ap[:, 0:256], num.ap[:, 256:512]
                        else:
                            num, den = ps[3 + nb % 3], ps[6 + nb % 2]
                            numap, denap = num.ap[:, 0:nq], den.ap[:, 0:nq]
                        nb += 1
                        kts = list(range(max(0, a0 // 128 - 1), (a0 + nq) // 128))
                        lastseg = (si == len(segs) - 1)

                        def post(h=h, gi=gi, res=res, a0=a0, dil=dil, nq=nq, num=num, den=den, nacc=nacc, dacc=dacc, ob=ob,
                                 lastseg=lastseg, numap=numap, denap=denap):
                            nv = ap3(nacc.ap[:, res + dil * a0:res + dil * a0 + 1], [[dil, nq]])
                            dv = ap3(dacc.ap[:, res + dil * a0:res + dil * a0 + 1], [[dil, nq]])
                            if gi == 0:
                                trk.op("vector", lambda hh: hh.tensor_copy(out=nv, in_=numap), reads=[num], writes=[nacc])
                                trk.op("scalar", lambda hh: hh.activation(out=dv, in_=denap, func=AF.Copy),
                                       reads=[den], writes=[dacc])
                            else:
                                trk.op("vector", lambda hh: hh.tensor_tensor(out=nv, in0=numap, in1=nv, op=ALU.add),
                                       reads=[num, nacc], writes=[nacc])
                                trk.op("vector", lambda hh: hh.tensor_tensor(out=dv, in0=denap, in1=dv, op=ALU.add),
                                       reads=[den, dacc], writes=[dacc])
                            if gi == 2 and lastseg:
                                trk.op("scalar", lambda hh: hh.activation(out=dacc.ap[:, :], in_=dacc.ap[:, :], func=AF.Ln),
                                       reads=[dacc], writes=[dacc])
                                trk.op("scalar", lambda hh: hh.activation(out=dacc.ap[:, :], in_=dacc.ap[:, :], func=AF.Exp,
                                                                          scale=-1.0), reads=[dacc], writes=[dacc])
                                trk.op("vector", lambda hh: hh.tensor_tensor(out=ob.ap[:, :], in0=nacc.ap[:, :], in1=dacc.ap[:, :],
                                                                             op=ALU.mult), reads=[nacc, dacc], writes=[ob])
                                trk.dma("sync", out=oT1[h], in_=ob.ap[:, :], reads=[ob])
                        for ti, kt in enumerate(kts):
                            j0 = a0 - 128 * kt + 384
                            d_ = kt - a0 // 128
                            off = max(0, 128 * d_)
                            n_ = min(nq, 128 * d_ + 256) - off
                            qa = ap3(Q.ap[:, res + dil * (a0 + off):res + dil * (a0 + off) + 1], [[dil, n_]])
                            nbase = 0
                            dbase = 256 if (num is den) else 0
                            t = dict(KT=ap3(kT.ap[:, res + dil * 128 * kt:res + dil * 128 * kt + 1], [[dil, 128]]), kb=kT,
                                     V=Vd[gi].ap[:, res * ntl + kt, :], vb=Vd[gi], E=B.ap[:, j0 + off:j0 + off + n_], eb=B,
                                     QT=Q, qap=qa, nq=n_, num=num, den=den,
                                     numap=num.ap[:, nbase + off:nbase + off + n_], denap=den.ap[:, dbase + off:dbase + off + n_],
                                     first=(ti == 0), last=(ti == len(kts) - 1))
                            if ti == len(kts) - 1:
                                t["post"] = post
                            if firsttask:
                                firsttask = False
                                if ci + 1 < len(combos):
                                    t["hook"] = (lambda c=combos[ci + 1], sl=(slot + 1) % 2: loadcombo(c[0], c[1], sl))
                            tasks.append(t)
                self.run_pipe(W, tasks)
            trk.barrier()

    def load_hT(self, src, hT, hparts):
        v = src.rearrange("h p t -> p h t")
        for i in range(16):
            self.trk.dma("sync", out=hT[:, :, i * 256:(i + 1) * 256], in_=v[:, :, i * 256:(i + 1) * 256], writes=[hparts[i]])

    def ffn(self, li, xin, xout, gname, w_up, cwd, w_dn, actT, ps):
        trk = self.trk
        with contextlib.ExitStack() as e1:
            hT = self.sb(e1, "f_hT", [128, 16, S + 2], BF16)
            hparts = [Buf(None) for _ in range(16)]
            z = Buf(None)
            trk.op("vector", lambda h: h.memset(hT[:, :, 0:2], 0.0), writes=[hparts[0]])
            self.phase_norm(xin, gname, hT, hparts, 2, ps[6:8])
            with contextlib.ExitStack() as e2:
                cw = Buf(self.sb(e2, "f_cw", [128, 88, 3], F32))
                trk.dma("sync", out=cw.ap[:, :, :], in_=cwd, writes=[cw])
                tg = [Buf(self.sb(e2, "f_tg%d" % i, [128, 456], F32)) for i in range(9)]
                tv = [Buf(self.sb(e2, "f_tv%d" % i, [128, 456], F32)) for i in range(2)]
                sg = [Buf(self.sb(e2, "f_sg%d" % i, [128, 456], F32)) for i in range(2)]
                ob = [Buf(self.sb(e2, "f_ob%d" % i, [128, 456], BF16)) for i in range(2)]
                toks = [(456 * i, min(458, S + 2 - 456 * i)) for i in range(9)]

                def epi(es_, c, m, tk, t0, n, p):
                    jb = c // 128
                    no = n - 2
                    isg = c < DFF
                    t_ = tg[tk] if isg else tv[tk % 2]
                    trk.op("scalar", lambda h: h.activation(out=t_.ap[:, 0:no], in_=p.ap[:, 2:n], func=AF.Copy,
                                                            scale=cw.ap[:, jb, 2:3]), reads=[p, cw], writes=[t_])
                    trk.op("vector", lambda h: h.scalar_tensor_tensor(out=t_.ap[:, 0:no], in0=p.ap[:, 1:n - 1],
                                                                      scalar=cw.ap[:, jb, 1:2], in1=t_.ap[:, 0:no],
                                                                      op0=ALU.mult, op1=ALU.add), reads=[p, cw, t_], writes=[t_])
                    trk.op("vector", lambda h: h.scalar_tensor_tensor(out=t_.ap[:, 0:no], in0=p.ap[:, 0:n - 2],
                                                                      scalar=cw.ap[:, jb, 0:1], in1=t_.ap[:, 0:no],
                                                                      op0=ALU.mult, op1=ALU.add), reads=[p, cw, t_], writes=[t_])
                    if not isg:
                        s_, o_, g_ = sg[tk % 2], ob[tk % 2], tg[tk]
                        trk.op("scalar", lambda h: h.activation(out=s_.ap[:, 0:no], in_=g_.ap[:, 0:no], func=AF.Silu),
                               reads=[g_], writes=[s_])
                        trk.op("gpsimd", lambda h: h.tensor_tensor(out=o_.ap[:, 0:no], in0=s_.ap[:, 0:no], in1=t_.ap[:, 0:no],
                                                                   op=ALU.mult), reads=[s_, t_], writes=[o_])
                        trk.dma("sync", out=actT[jb - 44, :, t0:t0 + no], in_=o_.ap[:, 0:no], reads=[o_])
                cols = [[(j * 128, 128), (DFF + j * 128, 128)] for j in range(44)]
                self.gemm_fm(hT, hparts, 2, 16, w_up, cols, toks, epi, ps[0:4], 256, "fu%d" % li)
        with contextlib.ExitStack() as e1:
            wbd = [Buf(self.sb(e1, "fd_w%d" % i, [128, 44, 512], BF16)) for i in range(2)]
            at = [Buf(self.sb(e1, "fd_a%d" % i, [128, 44, 512], BF16)) for i in range(2)]
            rsd = self.make_resid_epi(e1, "fd%d" % li, xin, xout)
            av = actT.rearrange("k p t -> p k t")
            Wv = w_dn.rearrange("(kc p) n -> p kc n", p=128)

            def loadw(ng):
                for k0 in range(0, 44, 11):
                    trk.dma("gpsimd", out=wbd[ng % 2].ap[:, k0:k0 + 11, :], in_=Wv[:, k0:k0 + 11, ng * 512:(ng + 1) * 512],
                            writes=[wbd[ng % 2]])

            def loada(i):
                tg = i % 8
                for k0 in range(0, 44, 11):
                    trk.dma("sync", out=at[i % 2].ap[:, k0:k0 + 11, :], in_=av[:, k0:k0 + 11, tg * 512:(tg + 1) * 512],
                            writes=[at[i % 2]])
            loadw(0)
            loada(0)
            k = 0
            pend = []
            for ng in range(4):
                if ng + 1 < 4:
                    loadw(ng + 1)
                for tg in range(8):
                    i = ng * 8 + tg
                    if i + 1 < 32:
                        loada(i + 1)
                    a_, w_ = at[i % 2], wbd[ng % 2]
                    for nb in range(4):
                        p = ps[k % 4]
                        k += 1
                        for kc in range(44):
                            trk.op("tensor", lambda h: h.matmul(p.ap[:, 0:512], lhsT=w_.ap[:, kc, nb * 128:(nb + 1) * 128],
                                                                rhs=a_.ap[:, kc, :], start=(kc == 0), stop=(kc == 43)),
                                   reads=[w_, a_], writes=[p], inc=(kc == 43))
                        if pend:
                            pend.pop()()
                        pend.append(lambda a=(None, ng * 512 + nb * 128, 128, 0, tg * 512, 512, p): rsd(*a))
            if pend:
                pend.pop()()
            trk.barrier()

    def build(self, upto=99):
        nc = self.nc
        din, dscr = self.din, self.dscr
        xT = din("xT", [D, S])
        rel = din("rel_bias", [32, 16])
        an = [din("attn_norm%d" % i, [128, 16]) for i in range(2)]
        fn = [din("ffn_norm%d" % i, [128, 16]) for i in range(2)]
        w_in = din("a_w_in", [D, A_IN])
        gq0 = din("gq0", [128, 4])
        cmp_pos = din("cmp_posT", [2, 128, 32])
        cmp_w1 = din("a_cmp_w1", [2, 4096, 128])
        cmp_w2 = din("a_cmp_w2", [2, 128, 128])
        a_w_out = din("a_w_out", [D, D])
        kvn = din("kv_norm", [128, 16])
        kv_w = din("kv_w", [D, 1024])
        gq1 = din("gq1", [128, 4])
        b_w_q = din("b_w_q", [D, 6144])
        b_w_out = din("b_w_out", [D, D])
        w_up = [din("ffn_w_up%d" % i, [D, 2 * DFF]) for i in range(2)]
        cw = [din("ffn_conv%d" % i, [128, 88, 3]) for i in range(2)]
        w_dn = [din("ffn_w_down%d" % i, [DFF, D]) for i in range(2)]
        oh = din("onehot", [5, 32, GLEN])
        cst = din("consts", [128, 1024])
        selc = din("selc", [128, 32 * 64])
        expand = din("expand", [128, 32 * 128])
        ovl = din("ovl", [128, 130])

        qT0 = dscr("qT0", [16, 128, S], BF16)
        kT_slc = dscr("kT_slc", [4, 128, S], BF16)
        kT_win = dscr("kT_win", [4, 128, S], BF16)
        kvraw = dscr("kvraw", [8, 128, S], BF16)
        v_slc = dscr("v_slc", [S, 512], BF16)
        v_win = dscr("v_win", [S, 512], BF16)
        gatesT = dscr("gatesT", [48, S], F32)
        oT0 = dscr("oT0", [16, 128, S], BF16)
        x1T = dscr("x1T", [D, S], F32)
        actT = dscr("actT", [44, 128, S], BF16)
        x2T = dscr("x2T", [D, S], F32)
        kcmpT = dscr("kcmpT", [4, 128, 256], BF16)
        vcmp = dscr("vcmp", [4, 128, 2, 128], BF16)
        kT_sh = dscr("kT_sh", [4, 128, S], BF16)
        v_sh = dscr("v_sh", [S, 512], BF16)
        qT1 = dscr("qT1", [48, 128, S], BF16)
        oT1 = dscr("oT1", [16, 128, S], BF16)
        x3T = dscr("x3T", [D, S], F32)
        outT = nc.dram_tensor("outT", [D, S], F32, kind="ExternalOutput").ap()

        with contextlib.ExitStack() as es:
            self.trk = trk = Trk(nc, es)
            ps = [Buf(es.enter_context(nc.psum_tensor("ps%d" % i, [128, 512], F32))) for i in range(8)]
            self.ones = Buf(self.sb(es, "ones", [128, 128], BF16))
            trk.op("vector", lambda h: h.memset(self.ones.ap[:, :], 1.0), writes=[self.ones])
            self.epsc = Buf(self.sb(es, "epsc", [128, 1], F32))
            trk.op("vector", lambda h: h.memset(self.epsc.ap[:, :], EPS), writes=[self.epsc])
            self.tiny = Buf(self.sb(es, "tiny", [128, 1], F32))
            trk.op("vector", lambda h: h.memset(self.tiny.ap[:, :], 1e-18), writes=[self.tiny])
            self.cD = Buf(self.sb(es, "cD", [128, 128], BF16))
            trk.op("vector", lambda h: h.memset(self.cD.ap[:, :], 1.0 / D), writes=[self.cD])
            self.cH = Buf(self.sb(es, "cH", [128, 128], BF16))
            trk.op("vector", lambda h: h.memset(self.cH.ap[:, :], 1.0 / DH), writes=[self.cH])

            tok8 = [(i * 512, 512) for i in range(8)]
            c512 = lambda n: [[(i * 512, 512)] for i in range(n)]
            self.bankgen(rel, oh, ps)
            with contextlib.ExitStack() as e1:
                hT = self.sb(e1, "hT", [128, 16, S + 2], BF16)
                hparts = [Buf(None) for _ in range(16)]
                self.phase_norm(xT, "attn_norm0", hT, hparts, 0, ps[6:8])
                with contextlib.ExitStack() as e2:
                    gq = Buf(self.sb(e2, "gq", [128, 4], F32))
                    trk.dma("sync", out=gq.ap[:, :], in_=gq0, writes=[gq])
                    hr = self.make_headrms_epi(e2, "p0", ps[4:6])
                    cp = self.make_copy_epi(e2, "p0c")
                    sg = self.make_copy_epi(e2, "p0g", dt=F32, func=AF.Sigmoid, nbuf=2)

                    def epi(es_, c, m, tk, t0, n, p):
                        if c < 2048:
                            hr(p, n, gq.ap[:, 0:1], gq, qT0[c // 128, :, t0:t0 + n])
                        elif c < 2048 + 1024:
                            cp(p, m, n, kvraw[(c - 2048) // 128, :, t0:t0 + n])
                        elif c < 2048 + 1536:
                            hr(p, n, gq.ap[:, 2:3], gq, kT_slc[(c - 3072) // 128, :, t0:t0 + n])
                        elif 4096 <= c < 4608:
                            hr(p, n, gq.ap[:, 3:4], gq, kT_win[(c - 4096) // 128, :, t0:t0 + n])
                        else:
                            sg(p, m, n, gatesT[0:m, t0:t0 + n])
                    cols = [[(i * 512, 512)] for i in (0, 1, 2, 3, 4, 5, 6, 8)] + [[(5120, 48)]]
                    self.gemm_fm(hT, hparts, 0, 16, w_in, cols, tok8, epi, ps[0:4], 512, "win")
                self.gemm_tm(hT, hparts, 0, w_in, 3584, v_slc, ps[0:4], "vs")
                self.gemm_tm(hT, hparts, 0, w_in, 4608, v_win, ps[0:4], "vw")
            if upto >= 2:
                with contextlib.ExitStack() as e2:
                    gq = Buf(self.sb(e2, "gqc", [128, 4], F32))
                    trk.dma("sync", out=gq.ap[:, :], in_=gq0, writes=[gq])
                    self.phase_compress(kvraw, cmp_w1, cmp_w2, cmp_pos, gq, kcmpT, vcmp, ps)
            while self.bgwork:
                self.bgwork.pop(0)()
            trk.barrier(allbg=True)
            if upto >= 3:
                self.phase_attn0(qT0, kT_slc, kT_win, v_slc, v_win, kcmpT, vcmp, gatesT, oT0, selc, expand, ovl, cst, ps)
            if upto >= 4:
                with contextlib.ExitStack() as e1:
                    hT = self.sb(e1, "hT", [128, 16, S + 2], BF16)
                    hparts = [Buf(None) for _ in range(16)]
                    self.load_hT(oT0, hT, hparts)
                    rsd = self.make_resid_epi(e1, "wo0", xT, x1T)
                    self.gemm_fm(hT, hparts, 0, 16, a_w_out, c512(4), tok8, rsd, ps[0:4], 512, "wo0")
            if upto >= 5:
                self.ffn(0, x1T, x2T, "ffn_norm0", w_up[0], cw[0], w_dn[0], actT, ps)
            if upto >= 6:
                with contextlib.ExitStack() as e1:
                    hT = self.sb(e1, "hT", [128, 16, S + 2], BF16)
                    hparts = [Buf(None) for _ in range(16)]
                    gq = Buf(self.sb(e1, "gq1", [128, 4], F32))
                    trk.dma("sync", out=gq.ap[:, :], in_=gq1, writes=[gq])
                    self.phase_norm(x2T, "kv_norm", hT, hparts, 0, ps[6:8])
                    with contextlib.ExitStack() as e2:
                        hr = self.make_headrms_epi(e2, "p1k", ps[4:6])

                        def epik(es_, c, m, tk, t0, n, p):
                            hr(p, n, gq.ap[:, 0:1], gq, kT_sh[c // 128, :, t0:t0 + n])
                        self.gemm_fm(hT, hparts, 0, 16, kv_w, c512(1), tok8, epik, ps[0:4], 512, "kvk")
                    self.gemm_tm(hT, hparts, 0, kv_w, 512, v_sh, ps[0:4], "vsh")
                    self.phase_norm(x2T, "attn_norm1", hT, hparts, 0, ps[6:8])
                    with contextlib.ExitStack() as e2:
                        hr = self.make_headrms_epi(e2, "p1q", ps[4:6])

                        def epiq(es_, c, m, tk, t0, n, p):
                            gi = c // 2048
                            hr(p, n, gq.ap[:, 1 + gi:2 + gi], gq, qT1[c // 128, :, t0:t0 + n])
                        self.gemm_fm(hT, hparts, 0, 16, b_w_q, c512(12), tok8, epiq, ps[0:4], 512, "bwq")
            if upto >= 7:
                self.phase_attn1(qT1, kT_sh, v_sh, oT1, ps)
            if upto >= 8:
                with contextlib.ExitStack() as e1:
                    hT = self.sb(e1, "hT", [128, 16, S + 2], BF16)
                    hparts = [Buf(None) for _ in range(16)]
                    self.load_hT(oT1, hT, hparts)
                    rsd = self.make_resid_epi(e1, "wo1", x2T, x3T)
                    self.gemm_fm(hT, hparts, 0, 16, b_w_out, c512(4), tok8, rsd, ps[0:4], 512, "wo1")
                self.ffn(1, x3T, outT, "ffn_norm1", w_up[1], cw[1], w_dn[1], actT, ps)
            trk.barrier()
        return nc


NEGV = 3840.0


def t5_bucket_np(n):
    n = np.maximum(np.asarray(n), 0).astype(np.int32)
    lr = np.log(np.maximum(n, 1).astype(np.float32) / np.float32(16)) / np.float32(math.log(4096 / 16))
    large = 16 + (lr * np.float32(16)).astype(np.int32)
    return np.where(n < 16, n, np.minimum(large, 31))


def host_consts():
    f = np.float32
    c = {}
    e = np.arange(GLEN)
    d = e - GOFF
    oh = np.zeros((5, 32, GLEN), f)
    specs = [(d >= 0, d), ((d >= 0) & (d <= 511), d), ((d >= 0) & (d <= 128), d),
             ((d >= 0) & (d <= 128), d * 4), ((d >= 0) & (d <= 128), d * 16)]
    for t, (valid, dist) in enumerate(specs):
        bk = t5_bucket_np(dist)
        oh[t, bk[valid], e[valid]] = 1.0
    c["onehot"] = oh
    cst = np.zeros((128, 1024), f)
    cst[:, 0:128] = np.eye(128, dtype=f)
    c["consts"] = cst
    p = np.arange(128)[:, None, None]
    i = np.arange(32)[None, :, None]
    blk = np.arange(64)[None, None, :]
    t = 128 * i + p
    cur = t // 64
    forced = (blk == 0) | (blk == cur) | (blk == cur - 1)
    causal = blk * 64 <= t
    sc = np.where(causal, np.where(forced, 1e4, 0.0), -1e30).astype(f)
    c["selc"] = np.ascontiguousarray(sc.reshape(128, 32 * 64))
    ex = np.zeros((128, 32, 128), f)
    for kt in range(32):
        ex[2 * kt, kt, 0:64] = NEGV
        ex[2 * kt + 1, kt, 64:128] = NEGV
    c["expand"] = ex.reshape(128, 32 * 128)
    n = np.arange(256)[:, None]
    j = np.arange(64)[None, :]
    ovm = np.maximum(np.minimum(16 * n + 32, 64 * j + 64) - np.maximum(16 * n, 64 * j), 0).astype(f) / 32.0
    ovm = np.concatenate([ovm, np.ones((256, 1), f)], axis=1)
    ovm[255] = 0.0
    c["ovl"] = np.ascontiguousarray(ovm.reshape(2, 128, 65).transpose(1, 0, 2).reshape(128, 130))
    return c


def host_all(inp, b):
    f = np.float32
    d = {}
    d["xT"] = np.ascontiguousarray(inp["x"][b].T)
    d["rel_bias"] = np.ascontiguousarray(inp["rel_bias"], f)

    def g16(v):
        return np.ascontiguousarray(np.asarray(v, f).reshape(16, 128).T)
    for i in range(2):
        d["attn_norm%d" % i] = g16(inp["attn_norm"][i])
        d["ffn_norm%d" % i] = g16(inp["ffn_norm"][i])
        d["ffn_w_up%d" % i] = inp["ffn_w_up"][i]
        d["ffn_w_down%d" % i] = inp["ffn_w_down"][i]
        d["ffn_conv%d" % i] = np.ascontiguousarray(inp["ffn_conv"][i].reshape(3, 88, 128).transpose(2, 1, 0))
    d["kv_norm"] = g16(inp["kv_norm"])
    d["a_w_in"] = inp["a_w_in"][0]
    kn = inp["a_k_norm"][0]
    d["gq0"] = np.ascontiguousarray(np.stack([inp["a_q_norm"][0], kn[0], kn[1], kn[2]], axis=1), f)
    d["cmp_posT"] = np.ascontiguousarray(inp["a_cmp_pos"][0].transpose(0, 2, 1))
    d["a_cmp_w1"] = inp["a_cmp_w1"][0]
    d["a_cmp_w2"] = inp["a_cmp_w2"][0]
    d["a_w_out"] = inp["a_w_out"][0]
    d["kv_w"] = inp["kv_w"]
    bq = inp["b_q_norm"][0]
    d["gq1"] = np.ascontiguousarray(np.stack([inp["kv_k_norm"], bq[0], bq[1], bq[2]], axis=1), f)
    d["b_w_q"] = inp["b_w_q"][0]
    d["b_w_out"] = inp["b_w_out"][0]
    d.update(host_consts())
    return d


_CACHE = {}


def kernel(**inp):
    inp = {k: np.asarray(v) for k, v in inp.items()}
    if "nc" not in _CACHE:
        kk = K()
        _CACHE["nc"] = kk.build()
    nc = _CACHE["nc"]
    maps = [host_all(inp, b) for b in range(8)]
    res = run_bass_kernel_spmd(nc, maps, core_ids=list(range(8)))
    out = np.empty((8, S, D), np.float32)
    for b in range(8):
        out[b] = np.asarray(res.results[b]["outT"]).T
    return out
```
